# Optimizing a Trainium2 kernel written in Bass

```python
import jax, jax.numpy as jnp
from jax import lax
import numpy as np

D_MODEL = 2048
BATCH = 4
SEQ = 2048
DEPTH = 1

PLE_DIM = 256
MIX_W = D_MODEL
HGRN_W = MIX_W // 2
CONV_W = MIX_W - HGRN_W
HGRN_HEAD_DIM = 128
HGRN_HEADS = HGRN_W // HGRN_HEAD_DIM
HGRN_CHUNK = 64
CONV_K = 3
CONV_GROUPS = 8
D_FF = 5632
EPS = 1e-6
MIX_IN = 4 * HGRN_W + 3 * CONV_W
SPLITS = [HGRN_W, 2 * HGRN_W, 3 * HGRN_W, 4 * HGRN_W,
          4 * HGRN_W + CONV_W, 4 * HGRN_W + 2 * CONV_W]

kernel_name = "hymba_hgrn2_shortconv_macaron"


def rms_norm(x, g):
    xf = x.astype(jnp.float32)
    r = lax.rsqrt(jnp.mean(xf * xf, axis=-1, keepdims=True) + EPS)
    return (xf * r).astype(x.dtype) * g


def swiglu(x, w_gate, w_up, w_down):
    return (jax.nn.silu(x @ w_gate) * (x @ w_up)) @ w_down


def hgrn2_chunked(q, f_logit, v, lb):
    bsz, s, h, dh = q.shape
    nc = s // HGRN_CHUNK
    qf = jax.nn.silu(q.astype(jnp.float32))
    f = lb + (1.0 - lb) * jax.nn.sigmoid(f_logit.astype(jnp.float32))
    k = 1.0 - f
    log_f = jnp.log(f)
    vf = v.astype(jnp.float32)

    def to_chunks(a):
        return a.reshape(bsz, nc, HGRN_CHUNK, h, dh).transpose(1, 0, 3, 2, 4)

    qc, kc, vc, lc = to_chunks(qf), to_chunks(k), to_chunks(vf), to_chunks(log_f)
    bc = jnp.cumsum(lc, axis=-2)
    causal = jnp.tril(jnp.ones((HGRN_CHUNK, HGRN_CHUNK), dtype=bool))

    def step(state, xs):
        qx, kx, vx, bx = xs
        rel = bx[..., :, None, :] - bx[..., None, :, :]
        decay = jnp.exp(jnp.where(causal[:, :, None], rel, -jnp.inf))
        scores = jnp.einsum('bhtk,bhsk,bhtsk->bhts', qx, kx, decay)
        o = (jnp.einsum('bhts,bhsv->bhtv', scores, vx)
             + jnp.einsum('bhtk,bhkv->bhtv', qx * jnp.exp(bx), state))
        b_last = bx[..., -1:, :]
        state = (jnp.exp(b_last)[..., 0, :, None] * state
                 + jnp.einsum('bhsk,bhsv->bhkv', kx * jnp.exp(b_last - bx), vx))
        return state, o

    s0 = jnp.zeros((bsz, h, dh, dh), jnp.float32)
    _, o = lax.scan(step, s0, (qc, kc, vc, bc))
    return o.transpose(1, 0, 3, 2, 4).reshape(bsz, s, h, dh).astype(q.dtype)


def short_conv(b_gate, c_gate, v, w_conv):
    u = c_gate * v
    up = jnp.pad(u, ((0, 0), (CONV_K - 1, 0), (0, 0)))
    s = u.shape[1]
    y = w_conv[0] * up[:, 0:s] + w_conv[1] * up[:, 1:s + 1] + w_conv[2] * up[:, 2:s + 2]
    return b_gate * y


def setup_inputs(seed: int = 0) -> dict:
    key = jax.random.key(seed)
    ks = jax.random.split(key, 24)

    def w(k, shape, fan_in):
        return jax.random.normal(k, shape, jnp.float32) * (fan_in ** -0.5)

    def gain(k, shape):
        return 1.0 + 0.02 * jax.random.normal(k, shape, jnp.float32)

    return {
        "x": jax.random.normal(ks[0], (BATCH, SEQ, D_MODEL), jnp.float32),
        "p": jax.random.normal(ks[1], (DEPTH, BATCH, SEQ, PLE_DIM), jnp.float32),
        "norm_ffn1": gain(ks[2], (DEPTH, D_MODEL)),
        "ffn1_gate": w(ks[3], (DEPTH, D_MODEL, D_FF), D_MODEL),
        "ffn1_up": w(ks[4], (DEPTH, D_MODEL, D_FF), D_MODEL),
        "ffn1_down": w(ks[5], (DEPTH, D_FF, D_MODEL), D_FF),
        "norm_mix": gain(ks[6], (DEPTH, D_MODEL)),
        "w_in": w(ks[7], (DEPTH, D_MODEL, MIX_IN), D_MODEL),
        "conv_w": w(ks[8], (DEPTH, CONV_K, CONV_W), CONV_K),
        "hgrn_lb_logits": jax.random.normal(ks[9], (DEPTH + 1, HGRN_W), jnp.float32),
        "hgrn_norm": gain(ks[10], (DEPTH, HGRN_HEAD_DIM)),
        "conv_norm": gain(ks[11], (DEPTH, CONV_W)),
        "w_out": w(ks[12], (DEPTH, MIX_W, D_MODEL), MIX_W),
        "norm_ffn2": gain(ks[13], (DEPTH, D_MODEL)),
        "ffn2_gate": w(ks[14], (DEPTH, D_MODEL, D_FF), D_MODEL),
        "ffn2_up": w(ks[15], (DEPTH, D_MODEL, D_FF), D_MODEL),
        "ffn2_down": w(ks[16], (DEPTH, D_FF, D_MODEL), D_FF),
        "norm_ple": gain(ks[17], (DEPTH, D_MODEL)),
        "w_ple": w(ks[18], (DEPTH, PLE_DIM, D_MODEL), PLE_DIM),
        "w_ple_gate": w(ks[19], (DEPTH, D_MODEL, D_MODEL), D_MODEL),
        "norm_final": gain(ks[20], (D_MODEL,)),
    }


def reference(x, p, norm_ffn1, ffn1_gate, ffn1_up, ffn1_down, norm_mix, w_in, conv_w,
              hgrn_lb_logits, hgrn_norm, conv_norm, w_out, norm_ffn2, ffn2_gate, ffn2_up,
              ffn2_down, norm_ple, w_ple, w_ple_gate, norm_final):
    bsz, s, _ = x.shape
    lb_all = jnp.cumsum(jax.nn.softmax(hgrn_lb_logits.astype(jnp.float32), axis=0), axis=0)
    h = x
    for i in range(DEPTH):
        h = h + 0.5 * swiglu(rms_norm(h, norm_ffn1[i]), ffn1_gate[i], ffn1_up[i], ffn1_down[i])

        u = rms_norm(h, norm_mix[i])
        proj = u @ w_in[i]
        q, f_logit, v_h, g_h, b_c, c_c, v_c = jnp.split(proj, SPLITS, axis=-1)

        hd = (bsz, s, HGRN_HEADS, HGRN_HEAD_DIM)
        lb = lb_all[i].reshape(HGRN_HEADS, HGRN_HEAD_DIM)
        o_h = hgrn2_chunked(q.reshape(hd), f_logit.reshape(hd), v_h.reshape(hd), lb)
        o_h = rms_norm(o_h, hgrn_norm[i]) * jax.nn.silu(g_h.reshape(hd))
        o_h = o_h.reshape(bsz, s, HGRN_W)

        y_c = short_conv(b_c, c_c, v_c, conv_w[i])
        y_c = rms_norm(y_c.reshape(bsz, s, CONV_GROUPS, CONV_W // CONV_GROUPS),
                       conv_norm[i].reshape(CONV_GROUPS, CONV_W // CONV_GROUPS)).reshape(bsz, s, CONV_W)

        h = h + jnp.concatenate([o_h, y_c], axis=-1) @ w_out[i]

        h = h + 0.5 * swiglu(rms_norm(h, norm_ffn2[i]), ffn2_gate[i], ffn2_up[i], ffn2_down[i])

        gate = jax.nn.sigmoid(rms_norm(h, norm_ple[i]) @ w_ple_gate[i])
        h = h + gate * (p[i] @ w_ple[i])
    return rms_norm(h, norm_final)
```

```python
import contextlib
import numpy as np
import concourse.bass as bass
import concourse.mybir as mybir
from concourse.bass_utils import run_bass_kernel_spmd

F32 = mybir.dt.float32
BF16 = mybir.dt.bfloat16
AF = mybir.ActivationFunctionType
ALU = mybir.AluOpType

ENGS = ("pe", "act", "dve", "pool", "sp")

D = 2048
T = 1024
TH = 512
NCH = 16
DFF = 5632
NFC = 44
GRP = 4
NGRP = 11
EPS = 1e-6
NCORES = 8


class Op:
    __slots__ = ("eng", "fn", "deps", "needs_inc", "inc_val", "dma_key", "dma_val", "dma_inc", "name")

    def __init__(self, eng, fn, name=""):
        self.eng = eng
        self.fn = fn
        self.deps = []
        self.needs_inc = False
        self.inc_val = None
        self.dma_key = None
        self.dma_val = None
        self.dma_inc = 16
        self.name = name


class Reg:
    __slots__ = ("space", "lo", "hi")

    def __init__(self, space, lo, hi):
        self.space = space
        self.lo = lo
        self.hi = hi


class Sched:
    def __init__(self):
        self.ops = {e: [] for e in ENGS}
        self.ent = {}
        self.dma_tot = {}
        self.dma_group = set()

    def _overl(self, reg):
        lst = self.ent.setdefault(reg.space, [])
        return [e for e in lst if e[0] < reg.hi and reg.lo < e[1]]

    def add(self, eng, fn, reads=(), writes=(), dma_key=None, dma_inc=16, name=""):
        op = Op(eng, fn, name)
        deps = {}
        for r in reads:
            for e in self._overl(r):
                if e[2] is not None:
                    deps[id(e[2])] = (e[2], "raw")
        for w in writes:
            for e in self._overl(w):
                if e[2] is not None and id(e[2]) not in deps:
                    deps[id(e[2])] = (e[2], "waw")
                for rd in e[3]:
                    if id(rd) not in deps:
                        deps[id(rd)] = (rd, "war")
        for d, kind in deps.values():
            if d is op:
                continue
            if d.dma_key is None and dma_key is None and d.eng == eng:
                if eng == "pe":
                    continue
            op.deps.append(d)
            d.needs_inc = True
        for w in writes:
            lst = self.ent.setdefault(w.space, [])
            keep = [e for e in lst if not (w.lo <= e[0] and e[1] <= w.hi)]
            keep.append([w.lo, w.hi, op, []])
            self.ent[w.space] = keep
        for r in reads:
            found = False
            for e in self._overl(r):
                e[3].append(op)
                if e[0] <= r.lo and r.hi <= e[1]:
                    found = True
            if not found:
                self.ent.setdefault(r.space, []).append([r.lo, r.hi, None, [op]])
        if dma_key is not None:
            op.dma_key = dma_key
            op.dma_inc = dma_inc
            self.dma_tot[dma_key] = self.dma_tot.get(dma_key, 0) + dma_inc
            op.dma_val = self.dma_tot[dma_key]
        self.ops[eng].append(op)
        return op

    def emit(self, block_engines, sems, dma_sems):
        for e in ENGS:
            n = 0
            for op in self.ops[e]:
                if op.dma_key is None and op.needs_inc:
                    n += 1
                    op.inc_val = n
        for key in self.dma_group:
            for e in ENGS:
                for op in self.ops[e]:
                    if op.dma_key == key:
                        op.dma_val = self.dma_tot[key]
        stats = {}

        def body_for(e):
            def body(eng):
                seen = {}
                nw = 0
                for op in self.ops[e]:
                    need = {}
                    for d in op.deps:
                        if d.dma_key is not None:
                            k, v = ("dma", d.dma_key), d.dma_val
                        else:
                            k, v = ("eng", d.eng), d.inc_val
                        if v > need.get(k, 0):
                            need[k] = v
                    for k, v in need.items():
                        if seen.get(k, 0) >= v:
                            continue
                        seen[k] = v
                        eng.wait_ge(dma_sems[k[1]] if k[0] == "dma" else sems[k[1]], v)
                        nw += 1
                    ins = op.fn(eng)
                    if ins is None:
                        continue
                    if op.dma_key is not None:
                        ins.then_inc(dma_sems[op.dma_key], op.dma_inc)
                    elif op.needs_inc:
                        ins.then_inc(sems[e], 1)
                stats[e] = (len(self.ops[e]), nw)
            return body

        for e in ENGS:
            if self.ops[e]:
                block_engines[e](body_for(e))
        return stats


class V:
    __slots__ = ("ap", "reg")

    def __init__(self, ap, reg):
        self.ap = ap
        self.reg = reg


def slab_plan():
    plan = []

    def ffn(tag):
        def gu(g):
            for fl in range(GRP):
                f = g * GRP + fl
                plan.append((tag + "_gate", "k", f))
                plan.append((tag + "_up", "k", f))

        def dn(g):
            for fl in range(GRP):
                plan.append((tag + "_down", "d", g * GRP + fl))
        gu(0)
        for g in range(1, NGRP):
            gu(g)
            dn(g - 1)
        dn(NGRP - 1)

    ffn("ffn1")
    mix_start = len(plan)
    for hd in range(8):
        plan.append(("w_in", "k", 0 + hd))
        plan.append(("w_in", "k", 8 + hd))
        plan.append(("w_in", "k", 16 + hd))
    for c in range(8):
        plan.append(("w_in", "k", 32 + c))
        plan.append(("w_in", "k", 40 + c))
        plan.append(("w_in", "k", 48 + c))
    for hd in range(8):
        plan.append(("w_in", "k", 24 + hd))
    for n in range(16):
        plan.append(("w_out", "k", n))
    mix_end = len(plan)
    ffn("ffn2")
    plan.append(("w_ple", "d", 0))
    plan.append(("w_ple", "d", 1))
    for n in range(16):
        plan.append(("w_ple_gate", "k", n))
    return plan, mix_start, mix_end


def build_slabs(plan, W):
    out = np.empty((len(plan), 128, 2048), np.float32)
    for i, (name, kind, idx) in enumerate(plan):
        w = W[name]
        if kind == "k":
            blk = w[:, idx * 128:(idx + 1) * 128]
            out[i] = blk.reshape(16, 128, 128).transpose(1, 0, 2).reshape(128, 2048)
        else:
            out[i] = w[idx * 128:(idx + 1) * 128, :]
    return out


PC_G1, PC_GM, PC_G2, PC_GP, PC_GF = 0, 16, 32, 48, 64
PC_L0, PC_L1 = 80, 88
PC_HGN = 96
PC_CVN = 97
PC_CW = 105
PC_FLAG = 129
PC_EPS = 130
NPRM_IN = 131
PC_LB, PC_OML, PC_NOML = 131, 139, 147
NPRM = 160
NCST = 512

A_PRM = 0
A_CSTB = 160
A_H = 416
A_X = A_H + 16384
A_RING = A_X + 8192
NSLOT = 12
NSLOT_MIX = 6
A_MIX = A_RING + NSLOT_MIX * 1024
A_PHASE = A_RING + NSLOT * 1024
A_END = 52000


class Builder:
    def __init__(self, debug=False, stop=None, nslab=None, skip1=False):
        self.debug = debug
        self.stop = stop
        self.skip1 = skip1
        self.used_slabs = None
        self.taps = []
        nc = bass.Bass("TRN2", target_bir_lowering=False)
        self.nc = nc
        self.S = Sched()
        self.plan, self.mix_start, self.mix_end = slab_plan()
        ns = len(self.plan) if nslab is None else nslab
        self.nslab = ns
        self.slab_off = self.mix_start if skip1 else 0
        self.d_x = nc.dram_tensor("xT", [128, 16 * T], F32, kind="ExternalInput")
        self.d_p = nc.dram_tensor("pT", [128, 2 * T], F32, kind="ExternalInput")
        self.d_prm = nc.dram_tensor("prm", [128, NPRM_IN], F32, kind="ExternalInput")
        self.d_cst = nc.dram_tensor("cst", [128, NCST], F32, kind="ExternalInput")
        self.d_w = nc.dram_tensor("wts", [ns - self.slab_off, 128, 2048], F32, kind="ExternalInput")
        self.d_y = nc.dram_tensor("yT", [128, 16 * T], F32, kind="ExternalOutput")
        self.d_s1 = nc.dram_tensor("send1", [128, 1024], F32)
        self.d_r1 = nc.dram_tensor("recv1", [256, 1024], F32)
        self.d_s2 = nc.dram_tensor("send2", [128, 16], F32)
        self.d_r2 = nc.dram_tensor("recv2", [256, 16], F32)
        self.stack = contextlib.ExitStack()
        self.sb = self.stack.enter_context(nc.sbuf_tensor("arena", [128, A_END], F32))
        self.ps = self.stack.enter_context(nc.psum_tensor("psum", [128, 4096], F32))
        self.cnt = 0

    def f32(self, lo, n):
        return V(self.sb[:, lo:lo + n], Reg("sb", lo, lo + n))

    def bf(self, lo, nwords):
        return V(self.sb[:, lo:lo + nwords].bitcast(BF16), Reg("sb", lo, lo + nwords))

    def bank(self, b):
        return V(self.ps[:, b * 512:(b + 1) * 512], Reg("ps", b * 512, (b + 1) * 512))

    def prm(self, col, n=1):
        return V(self.sb[:, A_PRM + col:A_PRM + col + n], Reg("sb", A_PRM + col, A_PRM + col + n))

    def op(self, eng, fn, reads=(), writes=(), **kw):
        return self.S.add(eng, fn, reads=[r.reg if isinstance(r, V) else r for r in reads],
                          writes=[w.reg if isinstance(w, V) else w for w in writes], **kw)

    def tap(self, name, v, ncols, dt=F32):
        if not self.debug:
            return
        d = self.nc.dram_tensor("dbg_" + name, [128, ncols], dt, kind="ExternalOutput")
        self.taps.append("dbg_" + name)
        step = 1024
        for c0 in range(0, ncols, step):
            c1 = min(ncols, c0 + step)
            self.op("sp", lambda e, c0=c0, c1=c1: e.dma_start(out=d[:, c0:c1], in_=v.ap[:, c0:c1]), reads=[v], dma_key="tap_" + name)
        self.S.dma_group.add("tap_" + name)

    def ring_init(self):
        self.r_free = list(range(NSLOT))
        self.r_next = self.slab_off
        self.r_hold = True
        self.r_slot = {}
        self.ring_fill()

    def slot_view(self, s):
        return self.bf(A_RING + s * 1024, 1024)

    def ring_fill(self):
        while self.r_next < self.nslab:
            i = self.r_next
            mix = self.mix_start <= i < self.mix_end
            cand = [s for s in self.r_free if (not mix) or s < NSLOT_MIX]
            if not cand:
                return
            s = cand[0]
            self.r_free.remove(s)
            self.r_slot[i] = s
            v = self.slot_view(s)
            src = self.d_w[i - self.slab_off]
            rd = [self.f32(A_H, 16384)] if (self.r_hold and i < self.slab_off + 3) else []
            self.op("pool", lambda e, v=v, src=src: e.dma_start(out=v.ap, in_=src), reads=rd, writes=[v], dma_key="r%d" % s)
            self.r_next += 1

    def ring_get(self, i):
        assert i in self.r_slot, "ring too small / order violated at slab %d (%s)" % (i, self.plan[i])
        return self.slot_view(self.r_slot[i])

    def ring_rel(self, i):
        self.r_free.append(self.r_slot.pop(i))
        self.r_free.sort()
        self.ring_fill()

    def mm(self, out_ap, out_v, lhsT_ap, rhs_ap, reads, start, stop):
        self.op("pe", lambda e: e.matmul(out_ap, lhsT=lhsT_ap, rhs=rhs_ap, start=start, stop=stop),
                reads=reads, writes=[out_v])

    def rmsnorm_sums(self, src_chunks, nparts_scale, sq_bufs, bsum):
        n = len(src_chunks)
        for c, sv in enumerate(src_chunks):
            sq = sq_bufs[c % len(sq_bufs)]
            if c % 2 == 0:
                self.op("act", lambda e, sq=sq, sv=sv: e.activation(out=sq.ap, in_=sv.ap, func=AF.Square), reads=[sv], writes=[sq])
            else:
                self.op("dve", lambda e, sq=sq, sv=sv: e.tensor_tensor(out=sq.ap, in0=sv.ap, in1=sv.ap, op=ALU.mult), reads=[sv], writes=[sq])
            for th in range(2):
                b = bsum[th]
                self.mm(b.ap, b, self.ONESB.ap, sq.ap[:, th * TH:(th + 1) * TH], [self.ONESB, sq], c == 0, c == n - 1)

    def norm_to_X(self, gcol):
        R = self.f32(self.a_R, 1024)
        sqs = [self.bf(self.a_SQ + i * 512, 512) for i in range(2)] + [self.bf(self.a_R + 1024 + i * 512, 512) for i in range(2)]
        bs = [self.bank(6), self.bank(7)]
        self.rmsnorm_sums([self.Hc(c) for c in range(16)], 1.0 / D, sqs, bs)
        eps = self.prm(PC_EPS)
        for th in range(2):
            self.op("act", lambda e, th=th: e.activation(out=R.ap[:, th * TH:(th + 1) * TH], in_=bs[th].ap, func=AF.Ln,
                                                         scale=1.0 / D, bias=eps.ap), reads=[bs[th], eps], writes=[R])
        self.op("act", lambda e: e.activation(out=R.ap, in_=R.ap, func=AF.Exp, scale=-0.5), reads=[R], writes=[R])
        for c in range(16):
            g = self.prm(gcol + c)
            xc = self.Xc(c)
            hc = self.Hc(c)
            self.op("dve", lambda e, g=g, xc=xc, hc=hc: e.scalar_tensor_tensor(
                out=xc.ap, in0=hc.ap, scalar=g.ap, in1=R.ap, op0=ALU.mult, op1=ALU.mult), reads=[hc, g, R], writes=[xc])

    def Hc(self, c):
        return self.f32(A_H + c * 1024, 1024)

    def Xc(self, c):
        return self.bf(A_X + c * 512, 512)

    def ffn(self, base, gcol):
        self.a_R = A_PHASE + 6144
        self.a_SQ = A_PHASE + 5120
        self.norm_to_X(gcol)
        HT = [self.bf(A_PHASE + i * 2048, 2048) for i in range(2)]
        T1 = [self.f32(A_PHASE + 4096 + i * 512, 512) for i in range(2)]
        st = {"gu": 0, "dn": 0, "slab": base}

        def take():
            i = st["slab"]
            st["slab"] += 1
            return i

        def gu(g):
            ht = HT[g % 2]
            for fl in range(GRP):
                ig, iu = take(), take()
                sg, su = self.ring_get(ig), self.ring_get(iu)
                sg3 = sg.ap.rearrange("p (k n) -> p k n", n=128)
                su3 = su.ap.rearrange("p (k n) -> p k n", n=128)
                for th in range(2):
                    u = st["gu"]
                    st["gu"] += 1
                    bg, bu = self.bank(2 * (u % 3)), self.bank(2 * (u % 3) + 1)
                    for kc in range(16):
                        xk = self.Xc(kc)
                        self.mm(bg.ap, bg, sg3[:, kc, :], xk.ap[:, th * TH:(th + 1) * TH], [sg, xk], kc == 0, kc == 15)
                    for kc in range(16):
                        xk = self.Xc(kc)
                        self.mm(bu.ap, bu, su3[:, kc, :], xk.ap[:, th * TH:(th + 1) * TH], [su, xk], kc == 0, kc == 15)
                    t1 = T1[u % 2]
                    self.op("act", lambda e, t1=t1, bg=bg: e.activation(out=t1.ap, in_=bg.ap, func=AF.Silu), reads=[bg], writes=[t1])
                    lo = fl * 1024 + th * TH
                    hv = V(ht.ap[:, lo:lo + TH], Reg("sb", ht.reg.lo + lo // 2, ht.reg.lo + (lo + TH) // 2))
                    self.op("dve", lambda e, hv=hv, t1=t1, bu=bu: e.tensor_tensor(out=hv.ap, in0=t1.ap, in1=bu.ap, op=ALU.mult),
                            reads=[t1, bu], writes=[hv])
                self.ring_rel(ig)
                self.ring_rel(iu)

        def dn(g):
            ht = HT[g % 2]
            ids = [take() for _ in range(GRP)]
            sl = [self.ring_get(i) for i in ids]
            for n in range(16):
                for th in range(2):
                    b = self.bank((6 + st["dn"]) % 8)
                    st["dn"] += 1
                    for fl in range(GRP):
                        lo = fl * 1024 + th * TH
                        self.mm(b.ap, b, sl[fl].ap[:, n * 128:(n + 1) * 128], ht.ap[:, lo:lo + TH], [sl[fl], ht], fl == 0, fl == GRP - 1)
                    hv = V(self.Hc(n).ap[:, th * TH:(th + 1) * TH], Reg("sb", A_H + n * 1024 + th * TH, A_H + n * 1024 + (th + 1) * TH))
                    self.op("dve", lambda e, hv=hv, b=b: e.scalar_tensor_tensor(
                        out=hv.ap, in0=b.ap, scalar=0.5, in1=hv.ap, op0=ALU.mult, op1=ALU.add), reads=[b, hv], writes=[hv])
            for i in ids:
                self.ring_rel(i)

        gu(0)
        for g in range(1, NGRP):
            gu(g)
            dn(g - 1)
        dn(NGRP - 1)
        return st["slab"]

    def build(self):
        S = self.S
        PRM = self.f32(A_PRM, NPRM_IN)
        self.op("sp", lambda e: e.dma_start(out=PRM.ap, in_=self.d_prm[:, :]), writes=[PRM], dma_key="ld")
        CSTB = self.bf(A_CSTB, 256)
        self.op("pool", lambda e: e.dma_start(out=CSTB.ap, in_=self.d_cst[:, :]), writes=[CSTB], dma_key="ldc")
        self.IDB = V(CSTB.ap[:, 0:128], CSTB.reg)
        self.ONESB = V(CSTB.ap[:, 128:256], CSTB.reg)
        self.MASK = V(CSTB.ap[:, 256:512], CSTB.reg)
        for c in range(16):
            hc = self.Hc(c)
            self.op("sp", lambda e, hc=hc, c=c: e.dma_start(out=hc.ap, in_=self.d_x[:, c * T:(c + 1) * T]), writes=[hc], dma_key="lx%d" % c)
        S.dma_group.add("ld")
        self.ring_init()
        LB, OML, NOML = self.prm(PC_LB, 8), self.prm(PC_OML, 8), self.prm(PC_NOML, 8)
        L0, L1 = self.prm(PC_L0, 8), self.prm(PC_L1, 8)
        self.op("dve", lambda e: e.tensor_tensor(out=LB.ap, in0=L0.ap, in1=L1.ap, op=ALU.subtract), reads=[L0, L1], writes=[LB])
        self.op("act", lambda e: e.activation(out=LB.ap, in_=LB.ap, func=AF.Sigmoid), reads=[LB], writes=[LB])
        self.op("dve", lambda e: e.tensor_scalar(out=OML.ap, in0=LB.ap, scalar1=-1.0, scalar2=1.0, op0=ALU.mult, op1=ALU.add), reads=[LB], writes=[OML])
        self.op("dve", lambda e: e.tensor_scalar(out=NOML.ap, in0=OML.ap, scalar1=-1.0, scalar2=None, op0=ALU.mult), reads=[OML], writes=[NOML])

        if self.skip1:
            nxt = self.mix_start
        else:
            nxt = self.ffn(0, PC_G1)
        self.tap("h1", self.f32(A_H, 16384), 16384)
        if self.stop == "ffn1":
            self.used_slabs = nxt
            return self.final()
        nxt = self.mixer(nxt)
        self.tap("h2", self.f32(A_H, 16384), 16384)
        if self.stop is not None and self.stop != "ffn2":
            self.used_slabs = max(nxt, self.r_next)
            return self.final()
        PT = self.bf(A_PHASE + 8192, 1024)
        self.op("pool", lambda e: e.dma_start(out=PT.ap, in_=self.d_p[:, :]), writes=[PT], dma_key="ldp")
        nxt = self.ffn(nxt, PC_G2)
        self.tap("h3", self.f32(A_H, 16384), 16384)
        if self.stop == "ffn2":
            self.used_slabs = nxt
            return self.final()
        nxt = self.ple(nxt, PT)
        assert nxt == len(self.plan)
        self.final()

    def ple(self, base, PT):
        self.norm_to_X(PC_GP)
        i0, i1 = base, base + 1
        sp = [self.ring_get(i0), self.ring_get(i1)]
        T1 = [self.f32(A_PHASE + 4096 + i * 512, 512) for i in range(2)]
        u = 0
        for n in range(16):
            i = base + 2 + n
            s = self.ring_get(i)
            s3 = s.ap.rearrange("p (k n) -> p k n", n=128)
            for th in range(2):
                ba, bb = self.bank(2 * (u % 3)), self.bank(2 * (u % 3) + 1)
                for kc in range(16):
                    xk = self.Xc(kc)
                    self.mm(ba.ap, ba, s3[:, kc, :], xk.ap[:, th * TH:(th + 1) * TH], [s, xk], kc == 0, kc == 15)
                for kc in range(2):
                    self.mm(bb.ap, bb, sp[kc].ap[:, n * 128:(n + 1) * 128], PT.ap[:, kc * T + th * TH:kc * T + (th + 1) * TH],
                            [sp[kc], PT], kc == 0, kc == 1)
                t1 = T1[u % 2]
                self.op("act", lambda e, t1=t1, ba=ba: e.activation(out=t1.ap, in_=ba.ap, func=AF.Sigmoid), reads=[ba], writes=[t1])
                self.op("dve", lambda e, t1=t1, bb=bb: e.tensor_tensor(out=t1.ap, in0=t1.ap, in1=bb.ap, op=ALU.mult), reads=[t1, bb], writes=[t1])
                hv = V(self.Hc(n).ap[:, th * TH:(th + 1) * TH], Reg("sb", A_H + n * 1024 + th * TH, A_H + n * 1024 + (th + 1) * TH))
                self.op("dve", lambda e, hv=hv, t1=t1: e.tensor_tensor(out=hv.ap, in0=hv.ap, in1=t1.ap, op=ALU.add), reads=[hv, t1], writes=[hv])
                u += 1
            self.ring_rel(i)
        self.ring_rel(i0)
        self.ring_rel(i1)
        return base + 18

    def final(self):
        R = self.f32(self.a_R, 1024)
        sqs = [self.bf(self.a_SQ + i * 512, 512) for i in range(2)]
        bs = [self.bank(6), self.bank(7)]
        self.rmsnorm_sums([self.Hc(c) for c in range(16)], 1.0 / D, sqs, bs)
        eps = self.prm(PC_EPS)
        for th in range(2):
            self.op("act", lambda e, th=th: e.activation(out=R.ap[:, th * TH:(th + 1) * TH], in_=bs[th].ap, func=AF.Ln,
                                                         scale=1.0 / D, bias=eps.ap), reads=[bs[th], eps], writes=[R])
        self.op("act", lambda e: e.activation(out=R.ap, in_=R.ap, func=AF.Exp, scale=-0.5), reads=[R], writes=[R])
        OB = [self.f32(A_PHASE + i * 1024, 1024) for i in range(4)]
        outs = []
        for c in range(16):
            g = self.prm(PC_GF + c)
            ob = OB[c % 4]
            hc = self.Hc(c)
            self.op("dve", lambda e, g=g, ob=ob, hc=hc: e.scalar_tensor_tensor(
                out=ob.ap, in0=hc.ap, scalar=g.ap, in1=R.ap, op0=ALU.mult, op1=ALU.mult), reads=[hc, g, R], writes=[ob])
            outs.append(self.op("sp", lambda e, ob=ob, c=c: e.dma_start(out=self.d_y[:, c * T:(c + 1) * T], in_=ob.ap),
                                reads=[ob], dma_key="out%d" % (c % 4)))
        fin = self.S.add("sp", lambda e: None, name="final")
        for e in ENGS:
            for op in self.S.ops[e]:
                if op.dma_key is not None and (op.dma_key.startswith("out") or op.dma_key.startswith("tap")):
                    fin.deps.append(op)

    def mixer(self, base):
        self.a_R = A_PHASE + 6144
        self.a_SQ = A_PHASE + 5120
        self.norm_to_X(PC_GM)
        a = A_MIX
        a_OLOC = a
        a += 8192
        a_QH = a
        a += 4096
        a_Y = a
        a += 4096
        a_SCR = a
        a_HSCR = a_Y
        assert A_END - a_SCR >= 4200

        def OLOC(hd, th):
            lo = a_OLOC + hd * 1024 + th * TH
            return self.f32(lo, TH)

        def QH(hd, th=None):
            if th is None:
                return self.bf(a_QH + hd * 512, 512)
            return self.bf(a_QH + hd * 512 + th * 256, 256)

        def Yv(c, th=None):
            if th is None:
                return self.bf(a_Y + c * 512, 512)
            return self.bf(a_Y + c * 512 + th * 256, 256)

        slab = {"i": base}

        def take():
            i = slab["i"]
            slab["i"] += 1
            return i

        p = a_HSCR
        sets = []
        for s in range(2):
            d = {}
            d["A2"] = self.f32(p, 512); p += 512
            d["A3"] = self.f32(p, 512); p += 512
            d["A4"] = self.f32(p, 512); p += 512
            d["QT"] = self.bf(p, 256); p += 256
            d["KT"] = self.bf(p, 256); p += 256
            d["KTM"] = self.bf(p, 256); p += 256
            d["VTM"] = self.bf(p, 256); p += 256
            d["S1B"] = self.bf(p, 512); p += 512
            d["PSB"] = self.bf(p, 256); p += 256
            d["DD"] = self.f32(p, 24); p += 24
            d["EE"] = self.f32(p, 24); p += 24
            sets.append(d)
        SR = []
        for i in range(4):
            SR.append(self.f32(p, 128)); p += 128
        SEND = self.f32(p, 1024); p += 1024
        assert p <= A_END, p
        hslabs = {}

        def hg_P(u):
            hd, th = divmod(u, 2)
            s = u % 2
            if th == 0:
                hslabs[hd] = (take(), take(), take())
            iq, if_, ii = hslabs[hd]
            sq, sf, si = self.ring_get(iq), self.ring_get(if_), self.ring_get(ii)
            sq3 = sq.ap.rearrange("p (k n) -> p k n", n=128)
            sf3 = sf.ap.rearrange("p (k n) -> p k n", n=128)
            si3 = si.ap.rearrange("p (k n) -> p k n", n=128)
            bq, bf_, bv = self.bank(4 * s), self.bank(4 * s + 1), self.bank(4 * s + 2)
            for kc in range(16):
                xk = self.Xc(kc)
                self.mm(bq.ap, bq, sq3[:, kc, :], xk.ap[:, th * TH:(th + 1) * TH], [sq, xk], kc == 0, kc == 15)
            for kc in range(16):
                xk = self.Xc(kc)
                self.mm(bf_.ap, bf_, sf3[:, kc, :], xk.ap[:, th * TH:(th + 1) * TH], [sf, xk], kc == 0, kc == 15)
            for tt in range(4):
                t0 = th * TH + tt * 128
                for kc in range(16):
                    xk = self.Xc(kc)
                    self.mm(bv.ap[:, tt * 128:(tt + 1) * 128], bv, xk.ap[:, t0:t0 + 128], si3[:, kc, :], [si, xk], kc == 0, kc == 15)
            if th == 1:
                for i in hslabs[hd]:
                    self.ring_rel(i)

        def hg_E(u):
            hd, th = divmod(u, 2)
            s = u % 2
            d = sets[s]
            dp = sets[1 - s]
            bq, bf_, bv = self.bank(4 * s), self.bank(4 * s + 1), self.bank(4 * s + 2)
            A2, A3, A4, QT, KT, VTM, DD, EE = d["A2"], d["A3"], d["A4"], d["QT"], d["KT"], d["VTM"], d["DD"], d["EE"]
            lb, oml, noml = self.prm(PC_LB + hd), self.prm(PC_OML + hd), self.prm(PC_NOML + hd)
            op = self.op
            op("act", lambda e: e.activation(out=bf_.ap, in_=bf_.ap, func=AF.Sigmoid), reads=[bf_], writes=[bf_])
            op("act", lambda e: e.activation(out=A4.ap, in_=bq.ap, func=AF.Sigmoid), reads=[bq], writes=[A4])
            op("act", lambda e: e.activation(out=A2.ap, in_=bf_.ap, func=AF.Ln, scale=oml.ap, bias=lb.ap), reads=[bf_, oml, lb], writes=[A2])
            op("dve", lambda e: e.tensor_tensor(out=bq.ap, in0=bq.ap, in1=A4.ap, op=ALU.mult), reads=[bq, A4], writes=[bq])
            op("dve", lambda e: e.tensor_scalar(out=bf_.ap, in0=bf_.ap, scalar1=noml.ap, scalar2=oml.ap, op0=ALU.mult, op1=ALU.add),
               reads=[bf_, noml, oml], writes=[bf_])
            if th == 0:
                op("dve", lambda e: e.tensor_tensor_scan(out=A3.ap, data0=A2.ap, data1=A2.ap, initial=0.0, op0=ALU.add, op1=ALU.bypass),
                   reads=[A2], writes=[A3])
            else:
                pa3 = dp["A3"]
                op("dve", lambda e: e.tensor_tensor_scan(out=A3.ap, data0=A2.ap, data1=A2.ap, initial=pa3.ap[:, 511:512], op0=ALU.add, op1=ALU.bypass),
                   reads=[A2, pa3], writes=[A3])
            A3v = A3.ap.rearrange("p (c t) -> p c t", t=64)
            A2v = A2.ap.rearrange("p (c t) -> p c t", t=64)
            op("dve", lambda e: e.tensor_tensor(out=A2v, in0=A3v, in1=A3v[:, :, 31:32].broadcast_to([128, 8, 64]), op=ALU.subtract),
               reads=[A3], writes=[A2])
            op("act", lambda e: e.activation(out=A4.ap, in_=A2.ap, func=AF.Exp), reads=[A2], writes=[A4])
            op("dve", lambda e: e.tensor_tensor(out=QT.ap, in0=bq.ap, in1=A4.ap, op=ALU.mult), reads=[bq, A4], writes=[QT])
            op("act", lambda e: e.activation(out=A4.ap, in_=A2.ap, func=AF.Exp, scale=-1.0), reads=[A2], writes=[A4])
            op("dve", lambda e: e.tensor_tensor(out=KT.ap, in0=bf_.ap, in1=A4.ap, op=ALU.mult), reads=[bf_, A4], writes=[KT])
            op("act", lambda e: e.activation(out=A4.ap, in_=A3.ap, func=AF.Exp), reads=[A3], writes=[A4])
            qh = QH(hd, th)
            op("dve", lambda e: e.tensor_tensor(out=qh.ap, in0=bq.ap, in1=A4.ap, op=ALU.mult), reads=[bq, A4], writes=[qh])
            DDv = DD.ap.rearrange("p (j c) -> p j c", c=8)
            Bref = A3v[:, :, 31]
            Blast = A3v[:, :, 63]
            if th == 0:
                op("dve", lambda e: e.tensor_copy(out=DDv[:, 0, 0:1], in_=Bref[:, 0:1]), reads=[A3], writes=[DD])
                op("dve", lambda e: e.tensor_copy(out=DDv[:, 2, 0:1], in_=Blast[:, 0:1]), reads=[A3], writes=[DD])
            else:
                pa3 = dp["A3"]
                op("dve", lambda e: e.tensor_tensor(out=DDv[:, 0, 0:1], in0=Bref[:, 0:1], in1=pa3.ap[:, 511:512], op=ALU.subtract), reads=[A3, pa3], writes=[DD])
                op("dve", lambda e: e.tensor_tensor(out=DDv[:, 2, 0:1], in0=Blast[:, 0:1], in1=pa3.ap[:, 511:512], op=ALU.subtract), reads=[A3, pa3], writes=[DD])
            op("dve", lambda e: e.tensor_tensor(out=DDv[:, 0, 1:8], in0=Bref[:, 1:8], in1=Blast[:, 0:7], op=ALU.subtract), reads=[A3], writes=[DD])
            op("dve", lambda e: e.tensor_tensor(out=DDv[:, 1, :], in0=Blast, in1=Bref, op=ALU.subtract), reads=[A3], writes=[DD])
            op("dve", lambda e: e.tensor_tensor(out=DDv[:, 2, 1:8], in0=Blast[:, 1:8], in1=Blast[:, 0:7], op=ALU.subtract), reads=[A3], writes=[DD])
            op("act", lambda e: e.activation(out=EE.ap, in_=DD.ap, func=AF.Exp), reads=[DD], writes=[EE])
            op("act", lambda e: e.activation(out=VTM.ap, in_=bv.ap, func=AF.Copy), reads=[bv], writes=[VTM])

        def hg_small(u):
            hd, th = divmod(u, 2)
            s = u % 2
            d = sets[s]
            bq, bf_, bv, bm = self.bank(4 * s), self.bank(4 * s + 1), self.bank(4 * s + 2), self.bank(4 * s + 3)
            QT, KT, KTM, VTM, S1B, PSB, EE = d["QT"], d["KT"], d["KTM"], d["VTM"], d["S1B"], d["PSB"], d["EE"]
            op = self.op
            EEv = EE.ap.rearrange("p (j c) -> p j c", c=8)
            bmb = bm.ap.bitcast(BF16)
            for tt in range(4):
                op("pe", lambda e, tt=tt: e.transpose(out=bmb[:, tt * 128:(tt + 1) * 128], in_=KT.ap[:, tt * 128:(tt + 1) * 128], identity=self.IDB.ap),
                   reads=[KT, self.IDB], writes=[bm])
            op("act", lambda e: e.activation(out=KTM.ap, in_=bmb[:, 0:512], func=AF.Copy), reads=[bm], writes=[KTM])
            if self.stop == "smA":
                return
            for c in range(8):
                tt, p0 = c // 2, (c % 2) * 64
                bo = bq if c % 2 == 0 else bf_
                op("pe", lambda e, bo=bo, c=c, tt=tt, p0=p0: e.matmul(bo.ap[:, tt * 128:(tt + 1) * 128],
                                                                    lhsT=KTM.ap[p0:p0 + 64, tt * 128:(tt + 1) * 128],
                                                                    rhs=VTM.ap[p0:p0 + 64, tt * 128:(tt + 1) * 128], start=True, stop=True),
                   reads=[KTM, VTM], writes=[bo])
            if self.stop == "smB1":
                return
            for c in range(8):
                p0 = (c % 2) * 64
                col = 256 + (c // 2) * 64
                op("pe", lambda e, c=c, p0=p0, col=col: e.matmul(bm.ap[p0:p0 + 64, col:col + 64], lhsT=KT.ap[:, c * 64:(c + 1) * 64],
                                                                rhs=QT.ap[:, c * 64:(c + 1) * 64], start=True, stop=True),
                   reads=[KT, QT], writes=[bm])
            PSBv = PSB.ap.rearrange("p (j t) -> p j t", t=128)
            bmv = bm.ap[:, 256:512].rearrange("p (j t) -> p j t", t=64)
            if u < 2:
                op("dve", lambda e: e.memset(PSB.ap, 0.0), writes=[PSB])
            for hf in range(2):
                r0 = hf * 64
                op("dve", lambda e, r0=r0: e.tensor_tensor(out=PSBv[r0:r0 + 64, :, r0:r0 + 64], in0=bmv[r0:r0 + 64, :, :],
                                                        in1=self.MASK.ap[r0:r0 + 64, 0:64].unsqueeze(1).broadcast_to([64, 4, 64]), op=ALU.mult),
                   reads=[bm, self.MASK], writes=[PSB])
            if self.stop == "smB":
                return
            if th == 0:
                op("dve", lambda e: e.memset(SR[3].ap, 0.0), writes=[SR[3]])
            S1v = S1B.ap.rearrange("p (c v) -> p c v", v=128)
            ELR = EEv[:, 1, :].rearrange("p (t two) -> p t two", two=2)
            for par, bo in ((0, bq), (1, bf_)):
                bov = bo.ap.rearrange("p (t v) -> p t v", v=128)
                op("dve", lambda e, par=par, bov=bov: e.tensor_tensor(out=bov, in0=bov, in1=ELR[:, :, par:par + 1].broadcast_to([128, 4, 128]), op=ALU.mult),
                   reads=[bo, EE], writes=[bo])
            for c in range(8):
                bo = bq if c % 2 == 0 else bf_
                pk = bo.ap[:, (c // 2) * 128:(c // 2 + 1) * 128]
                prev, cur = SR[(c - 1) % 4], SR[c % 4]
                op("act", lambda e, c=c, prev=prev: e.activation(out=S1v[:, c, :], in_=prev.ap, func=AF.Copy, scale=EEv[:, 0, c:c + 1]), reads=[prev, EE], writes=[S1B])
                op("dve", lambda e, c=c, pk=pk, prev=prev, cur=cur: e.scalar_tensor_tensor(out=cur.ap, in0=prev.ap, scalar=EEv[:, 2, c:c + 1], in1=pk, op0=ALU.mult, op1=ALU.add),
                   reads=[prev, EE, bo], writes=[cur])
            if self.stop == "smC":
                return
            for tt in range(4):
                op("pe", lambda e, tt=tt: e.matmul(bv.ap[:, tt * 128:(tt + 1) * 128], lhsT=VTM.ap[:, tt * 128:(tt + 1) * 128],
                                                  rhs=PSB.ap[:, tt * 128:(tt + 1) * 128], start=True, stop=False),
                   reads=[VTM, PSB], writes=[bv])
                for c in (2 * tt, 2 * tt + 1):
                    op("pe", lambda e, c=c: e.matmul(bv.ap[:, c * 64:(c + 1) * 64], lhsT=S1v[:, c, :], rhs=QT.ap[:, c * 64:(c + 1) * 64],
                                                    start=False, stop=(c % 2 == 1)),
                       reads=[S1B, QT], writes=[bv])
            ol = OLOC(hd, th)
            op("act", lambda e: e.activation(out=ol.ap, in_=bv.ap, func=AF.Copy), reads=[bv], writes=[ol])
            if th == 1:
                sv = V(SEND.ap[:, hd * 128:(hd + 1) * 128], Reg("sb", SEND.reg.lo + hd * 128, SEND.reg.lo + (hd + 1) * 128))
                op("dve", lambda e: e.tensor_copy(out=sv.ap, in_=SR[3].ap), reads=[SR[3]], writes=[sv])

        hg_P(0)
        if self.stop == "hgP0":
            for i in hslabs[0]:
                self.ring_rel(i)
            return slab["i"]
        hg_E(0)
        if self.stop == "hgE0":
            self.tap("QT0", sets[0]["QT"], 512, BF16)
            self.tap("KT0", sets[0]["KT"], 512, BF16)
            self.tap("A30", sets[0]["A3"], 512)
            self.tap("EE0", sets[0]["EE"], 24)
            self.tap("VTM0", sets[0]["VTM"], 512, BF16)
            for i in hslabs[0]:
                self.ring_rel(i)
            return slab["i"]
        nun = 16
        if self.stop is not None and self.stop.startswith("hgU"):
            nun = int(self.stop[3:])
        if self.stop in ("smA", "smB", "smB1", "smC", "smD"):
            hg_small(0)
            self.tap("KTM0", sets[0]["KTM"], 512, BF16)
            self.tap("PSB0", sets[0]["PSB"], 256, BF16)
            self.tap("S1B0", sets[0]["S1B"], 1024, BF16)
            self.tap("Sst", SR[3], 128)
            self.tap("oloc", self.f32(a_OLOC, 8192), 8192)
            for i in hslabs[0]:
                self.ring_rel(i)
            return slab["i"]
        for u in range(nun):
            if u + 1 < 16:
                hg_P(u + 1)
            hg_small(u)
            if u + 1 < 16:
                hg_E(u + 1)
        if nun < 16:
            self.tap("oloc", self.f32(a_OLOC, 8192), 8192)
            for hd in list(hslabs.keys()):
                if 2 * hd + 1 > nun:
                    for i in hslabs[hd]:
                        if i in self.r_slot:
                            self.ring_rel(i)
            return slab["i"]
        self.tap("oloc", self.f32(a_OLOC, 8192), 8192)
        self.tap("send", SEND, 1024)
        if self.stop == "hgrn":
            return slab["i"]
        rs1, rr1 = Reg("d_s1", 0, 1), Reg("d_r1", 0, 1)
        self.op("sp", lambda e: e.dma_start(out=self.d_s1[:, :], in_=SEND.ap), reads=[SEND], writes=[rs1], dma_key="snd1")
        self.op("pool", lambda e: e.collective_compute("AllGather", ALU.bypass, replica_groups=[[0, 1], [2, 3], [4, 5], [6, 7]],
                                                       ins=[self.d_s1.ap().opt()], outs=[self.d_r1.ap().opt()]),
                reads=[rs1], writes=[rr1], dma_key="cc1", dma_inc=1)

        if self.stop == "x1":
            RECVt = self.f32(a_SCR, 1024)
            self.op("sp", lambda e: e.dma_start(out=RECVt.ap, in_=self.d_r1[0:128, :]), reads=[rr1], writes=[RECVt], dma_key="rcv1")
            self.tap("recv", RECVt, 1024)
            return slab["i"]
        p = a_SCR
        csets = []
        for s in range(2):
            d = {}
            d["VS"] = self.f32(p, 512); p += 512
            d["U2"] = self.f32(p, 514); p += 514
            d["YA"] = self.f32(p, 512); p += 512
            d["SQ"] = self.bf(p, 256); p += 256
            csets.append(d)
        YPRE = self.f32(p, 16); p += 16
        BH = self.f32(p, 16); p += 16
        SEND2 = self.f32(p, 16); p += 16
        assert p <= A_END, p
        YPREv = YPRE.ap.rearrange("p (c t) -> p c t", t=2)
        BHv = BH.ap.rearrange("p (c t) -> p c t", t=2)
        SEND2v = SEND2.ap.rearrange("p (c t) -> p c t", t=2)
        cslabs = {}
        eps = self.prm(PC_EPS)

        def cv_P(u):
            c, th = divmod(u, 2)
            s = u % 2
            if th == 0:
                cslabs[c] = (take(), take(), take())
            ids = cslabs[c]
            for j in range(3):
                sl = self.ring_get(ids[j])
                s3 = sl.ap.rearrange("p (k n) -> p k n", n=128)
                b = self.bank(4 * s + j)
                for kc in range(16):
                    xk = self.Xc(kc)
                    self.mm(b.ap, b, s3[:, kc, :], xk.ap[:, th * TH:(th + 1) * TH], [sl, xk], kc == 0, kc == 15)
            if th == 1:
                for i in ids:
                    self.ring_rel(i)

        def cv_E(u):
            c, th = divmod(u, 2)
            s = u % 2
            d, dp = csets[s], csets[1 - s]
            bB, bC, bv, bn = self.bank(4 * s), self.bank(4 * s + 1), self.bank(4 * s + 2), self.bank(4 * s + 3)
            VS, U2, YA, SQ = d["VS"], d["U2"], d["YA"], d["SQ"]
            w0, w1, w2 = self.prm(PC_CW + c), self.prm(PC_CW + 8 + c), self.prm(PC_CW + 16 + c)
            cvn = self.prm(PC_CVN + c)
            op = self.op
            op("act", lambda e: e.activation(out=VS.ap, in_=bv.ap, func=AF.Copy), reads=[bv], writes=[VS])
            if th == 0:
                op("dve", lambda e: e.memset(U2.ap[:, 0:2], 0.0), writes=[U2])
            else:
                pu = dp["U2"]
                op("dve", lambda e: e.tensor_copy(out=U2.ap[:, 0:2], in_=pu.ap[:, 512:514]), reads=[pu], writes=[U2])
            op("dve", lambda e: e.tensor_tensor(out=U2.ap[:, 2:514], in0=bC.ap, in1=VS.ap, op=ALU.mult), reads=[bC, VS], writes=[U2])
            op("dve", lambda e: e.tensor_scalar(out=YA.ap, in0=U2.ap[:, 0:512], scalar1=w0.ap, scalar2=None, op0=ALU.mult), reads=[U2, w0], writes=[YA])
            op("dve", lambda e: e.scalar_tensor_tensor(out=YA.ap, in0=U2.ap[:, 1:513], scalar=w1.ap, in1=YA.ap, op0=ALU.mult, op1=ALU.add), reads=[U2, w1, YA], writes=[YA])
            op("dve", lambda e: e.scalar_tensor_tensor(out=YA.ap, in0=U2.ap[:, 2:514], scalar=w2.ap, in1=YA.ap, op0=ALU.mult, op1=ALU.add), reads=[U2, w2, YA], writes=[YA])
            if th == 0:
                op("dve", lambda e: e.tensor_copy(out=YPREv[:, c, :], in_=YA.ap[:, 0:2]), reads=[YA], writes=[YPRE])
                op("dve", lambda e: e.tensor_copy(out=BHv[:, c, :], in_=bB.ap[:, 0:2]), reads=[bB], writes=[BH])
            else:
                op("dve", lambda e: e.tensor_copy(out=SEND2v[:, c, :], in_=U2.ap[:, 512:514]), reads=[U2], writes=[SEND2])
            op("dve", lambda e: e.tensor_tensor(out=YA.ap, in0=YA.ap, in1=bB.ap, op=ALU.mult), reads=[YA, bB], writes=[YA])
            op("act", lambda e: e.activation(out=SQ.ap, in_=YA.ap, func=AF.Square), reads=[YA], writes=[SQ])
            self.mm(bn.ap, bn, self.ONESB.ap, SQ.ap, [self.ONESB, SQ], True, True)
            op("act", lambda e: e.activation(out=VS.ap, in_=bn.ap, func=AF.Ln, scale=1.0 / 128, bias=eps.ap), reads=[bn, eps], writes=[VS])
            op("act", lambda e: e.activation(out=VS.ap, in_=VS.ap, func=AF.Exp, scale=-0.5), reads=[VS], writes=[VS])
            yv = Yv(c, th)
            op("dve", lambda e: e.scalar_tensor_tensor(out=yv.ap, in0=YA.ap, scalar=cvn.ap, in1=VS.ap, op0=ALU.mult, op1=ALU.mult), reads=[YA, cvn, VS], writes=[yv])

        cv_P(0)
        for u in range(16):
            if u + 1 < 16:
                cv_P(u + 1)
            cv_E(u)
        if self.stop == "conv":
            return slab["i"]
        rs2, rr2 = Reg("d_s2", 0, 1), Reg("d_r2", 0, 1)
        self.op("sp", lambda e: e.dma_start(out=self.d_s2[:, :], in_=SEND2.ap), reads=[SEND2], writes=[rs2], dma_key="snd2")
        self.op("pool", lambda e: e.collective_compute("AllGather", ALU.bypass, replica_groups=[[0, 1], [2, 3], [4, 5], [6, 7]],
                                                       ins=[self.d_s2.ap().opt()], outs=[self.d_r2.ap().opt()]),
                reads=[rs2], writes=[rr2], dma_key="cc2", dma_inc=1)

        p = a_SCR
        psets = []
        for s in range(4):
            d = {}
            d["SQ"] = self.bf(p, 256); p += 256
            psets.append(d)
        SINB = self.bf(p, 512); p += 512
        RECV = self.f32(p, 1024); p += 1024
        RECV2 = self.f32(p, 16); p += 16
        UP = self.f32(p, 16); p += 16
        YF = self.f32(p, 16); p += 16
        TQ = self.f32(p, 24); p += 24
        RH = self.f32(p, 16); p += 16
        SQH = self.bf(p, 8); p += 8
        assert p <= YPRE.reg.lo, (p, YPRE.reg.lo)
        flag = self.prm(PC_FLAG)
        hgn = self.prm(PC_HGN)
        self.op("sp", lambda e: e.dma_start(out=RECV.ap, in_=self.d_r1[0:128, :]), reads=[rr1], writes=[RECV], dma_key="rcv1")
        SINv = SINB.ap.rearrange("p (h v) -> p h v", v=128)
        for hd in range(8):
            self.op("act", lambda e, hd=hd: e.activation(out=SINv[:, hd, :], in_=RECV.ap[:, hd * 128:(hd + 1) * 128], func=AF.Copy, scale=flag.ap),
                    reads=[RECV, flag], writes=[SINB])
        gsl = {}

        def po_P(u):
            hd, th = divmod(u, 2)
            s = u % 2
            if th == 0:
                gsl[hd] = take()
            sl = self.ring_get(gsl[hd])
            s3 = sl.ap.rearrange("p (k n) -> p k n", n=128)
            s = u % 4
            bg, bc = self.bank(2 * s), self.bank(2 * s + 1)
            for kc in range(16):
                xk = self.Xc(kc)
                self.mm(bg.ap, bg, s3[:, kc, :], xk.ap[:, th * TH:(th + 1) * TH], [sl, xk], kc == 0, kc == 15)
            qh = QH(hd, th)
            self.mm(bc.ap, bc, SINv[:, hd, :], qh.ap, [SINB, qh], True, True)
            if th == 1:
                self.ring_rel(gsl[hd])

        def po_E(u, part):
            hd, th = divmod(u, 2)
            s = u % 2
            s = u % 4
            d = psets[s]
            SQ = d["SQ"]
            bg, bc = self.bank(2 * s), self.bank(2 * s + 1)
            bn = bc
            R1 = bc
            ol = OLOC(hd, th)
            qh = QH(hd, th)
            op = self.op
            if part == 1:
                op("act", lambda e: e.activation(out=bg.ap, in_=bg.ap, func=AF.Silu), reads=[bg], writes=[bg])
                op("dve", lambda e: e.tensor_tensor(out=ol.ap, in0=ol.ap, in1=bc.ap, op=ALU.add), reads=[ol, bc], writes=[ol])
                op("act", lambda e: e.activation(out=SQ.ap, in_=ol.ap, func=AF.Square), reads=[ol], writes=[SQ])
                self.mm(bn.ap, bn, self.ONESB.ap, SQ.ap, [self.ONESB, SQ], True, True)
                return
            op("act", lambda e: e.activation(out=R1.ap, in_=bn.ap, func=AF.Ln, scale=1.0 / 128, bias=eps.ap), reads=[bn, eps], writes=[R1])
            op("act", lambda e: e.activation(out=R1.ap, in_=R1.ap, func=AF.Exp, scale=-0.5), reads=[R1], writes=[R1])
            op("dve", lambda e: e.scalar_tensor_tensor(out=ol.ap, in0=ol.ap, scalar=hgn.ap, in1=R1.ap, op0=ALU.mult, op1=ALU.mult), reads=[ol, hgn, R1], writes=[ol])
            op("dve", lambda e: e.tensor_tensor(out=qh.ap, in0=ol.ap, in1=bg.ap, op=ALU.mult), reads=[ol, bg], writes=[qh])

        for u in range(3):
            po_P(u)
        po_E(0, 1)
        for u in range(16):
            if u + 3 < 16:
                po_P(u + 3)
            if u + 1 < 16:
                po_E(u + 1, 1)
            po_E(u, 2)

        if self.stop == "post":
            return slab["i"]
        op = self.op
        op("sp", lambda e: e.dma_start(out=RECV2.ap, in_=self.d_r2[0:128, :]), reads=[rr2], writes=[RECV2], dma_key="rcv2")
        op("dve", lambda e: e.tensor_scalar(out=UP.ap, in0=RECV2.ap, scalar1=flag.ap, scalar2=None, op0=ALU.mult), reads=[RECV2, flag], writes=[UP])
        UPv = UP.ap.rearrange("p (c t) -> p c t", t=2)
        YFv = YF.ap.rearrange("p (c t) -> p c t", t=2)
        TQv = TQ.ap.rearrange("p (j c) -> p j c", c=8)
        CW0 = self.prm(PC_CW, 8)
        CW1 = self.prm(PC_CW + 8, 8)
        CVN = self.prm(PC_CVN, 8)
        op("dve", lambda e: e.tensor_tensor(out=TQv[:, 0, :], in0=UPv[:, :, 0], in1=CW0.ap, op=ALU.mult), reads=[UP, CW0], writes=[TQ])
        op("dve", lambda e: e.tensor_tensor(out=TQv[:, 1, :], in0=UPv[:, :, 1], in1=CW1.ap, op=ALU.mult), reads=[UP, CW1], writes=[TQ])
        op("dve", lambda e: e.tensor_tensor(out=TQv[:, 2, :], in0=UPv[:, :, 1], in1=CW0.ap, op=ALU.mult), reads=[UP, CW0], writes=[TQ])
        op("dve", lambda e: e.tensor_tensor(out=YFv[:, :, 0], in0=YPREv[:, :, 0], in1=TQv[:, 0, :], op=ALU.add), reads=[YPRE, TQ], writes=[YF])
        op("dve", lambda e: e.tensor_tensor(out=YFv[:, :, 0], in0=YFv[:, :, 0], in1=TQv[:, 1, :], op=ALU.add), reads=[YF, TQ], writes=[YF])
        op("dve", lambda e: e.tensor_tensor(out=YFv[:, :, 1], in0=YPREv[:, :, 1], in1=TQv[:, 2, :], op=ALU.add), reads=[YPRE, TQ], writes=[YF])
        op("dve", lambda e: e.tensor_tensor(out=YF.ap, in0=YF.ap, in1=BH.ap, op=ALU.mult), reads=[YF, BH], writes=[YF])
        op("act", lambda e: e.activation(out=SQH.ap, in_=YF.ap, func=AF.Square), reads=[YF], writes=[SQH])
        bh = self.bank(3)
        self.mm(bh.ap[:, 0:16], bh, self.ONESB.ap, SQH.ap, [self.ONESB, SQH], True, True)
        op("act", lambda e: e.activation(out=RH.ap, in_=bh.ap[:, 0:16], func=AF.Ln, scale=1.0 / 128, bias=eps.ap), reads=[bh, eps], writes=[RH])
        op("act", lambda e: e.activation(out=RH.ap, in_=RH.ap, func=AF.Exp, scale=-0.5), reads=[RH], writes=[RH])
        op("dve", lambda e: e.tensor_tensor(out=YF.ap, in0=YF.ap, in1=RH.ap, op=ALU.mult), reads=[YF, RH], writes=[YF])
        op("dve", lambda e: e.tensor_tensor(out=YFv, in0=YFv, in1=CVN.ap.unsqueeze(2).broadcast_to([128, 8, 2]), op=ALU.mult), reads=[YF, CVN], writes=[YF])
        Yall = self.bf(a_Y, 4096)
        Yallv = Yall.ap.rearrange("p (c t) -> p c t", t=1024)
        op("dve", lambda e: e.tensor_copy(out=Yallv[:, :, 0:2], in_=YFv), reads=[YF], writes=[Yall])
        self.tap("ycv", Yall, 8192, BF16)
        self.tap("ohg", self.bf(a_QH, 4096), 8192, BF16)

        if self.stop == "halo":
            return slab["i"]
        bi = 0
        for n in range(16):
            i = take()
            sl = self.ring_get(i)
            s3 = sl.ap.rearrange("p (k n) -> p k n", n=128)
            for th in range(2):
                b = self.bank(bi % 8)
                bi += 1
                for kc in range(16):
                    src = QH(kc, th) if kc < 8 else Yv(kc - 8, th)
                    self.mm(b.ap, b, s3[:, kc, :], src.ap, [sl, src], kc == 0, kc == 15)
                hv = V(self.Hc(n).ap[:, th * TH:(th + 1) * TH], Reg("sb", A_H + n * 1024 + th * TH, A_H + n * 1024 + (th + 1) * TH))
                self.op("dve", lambda e, hv=hv, b=b: e.tensor_tensor(out=hv.ap, in0=hv.ap, in1=b.ap, op=ALU.add), reads=[hv, b], writes=[hv])
            self.ring_rel(i)
        return slab["i"]

    def finish(self):
        nc = self.nc
        with contextlib.ExitStack() as st:
            sems = {e: st.enter_context(nc.semaphore("s_" + e)) for e in ENGS}
            dsem = {k: st.enter_context(nc.semaphore("d_" + k)) for k in self.S.dma_tot.keys()}
            block = st.enter_context(nc.Block())
            be = {"pe": block.tensor, "act": block.scalar, "dve": block.vector, "pool": block.gpsimd, "sp": block.sync}
            self.stats = self.S.emit(be, sems, dsem)
        self.stack.close()
        return nc


def _consts():
    cst = np.zeros((128, NCST), np.float32)
    cst[:, 0:128] = np.eye(128, dtype=np.float32)
    cst[:, 128:256] = 1.0
    pp = np.arange(128)[:, None] % 64
    tt = np.arange(256)[None, :] % 64
    cst[:, 256:512] = (pp <= tt).astype(np.float32)
    return cst


def _prm(I, core):
    prm = np.zeros((128, NPRM_IN), np.float32)

    def col16(v):
        return np.asarray(v, np.float32).reshape(-1, 128).T

    prm[:, PC_G1:PC_G1 + 16] = col16(I["norm_ffn1"][0])
    prm[:, PC_GM:PC_GM + 16] = col16(I["norm_mix"][0])
    prm[:, PC_G2:PC_G2 + 16] = col16(I["norm_ffn2"][0])
    prm[:, PC_GP:PC_GP + 16] = col16(I["norm_ple"][0])
    prm[:, PC_GF:PC_GF + 16] = col16(I["norm_final"])
    prm[:, PC_L0:PC_L0 + 8] = col16(I["hgrn_lb_logits"][0])
    prm[:, PC_L1:PC_L1 + 8] = col16(I["hgrn_lb_logits"][1])
    prm[:, PC_HGN] = np.asarray(I["hgrn_norm"][0], np.float32)
    prm[:, PC_CVN:PC_CVN + 8] = col16(I["conv_norm"][0])
    for j in range(3):
        prm[:, PC_CW + 8 * j:PC_CW + 8 * j + 8] = col16(I["conv_w"][0][j])
    prm[:, PC_FLAG] = float(core % 2)
    prm[:, PC_EPS] = EPS
    return prm


_CACHE = {}


STOP = None
SKIP1 = False


def _get_program(debug=False):
    key = ("prog", debug, STOP, SKIP1)
    if key not in _CACHE:
        ns = None
        if STOP is not None:
            b0 = Builder(debug=False, stop=STOP, skip1=SKIP1)
            b0.build()
            ns = b0.used_slabs
            b0.stack.close()
        b = Builder(debug=debug, stop=STOP, nslab=ns, skip1=SKIP1)
        b.build()
        b.finish()
        _CACHE[key] = b
    return _CACHE[key]


def kernel(x, p, norm_ffn1, ffn1_gate, ffn1_up, ffn1_down, norm_mix, w_in, conv_w, hgrn_lb_logits, hgrn_norm,
           conv_norm, w_out, norm_ffn2, ffn2_gate, ffn2_up, ffn2_down, norm_ple, w_ple, w_ple_gate, norm_final,
           _debug=False):
    I = dict(norm_ffn1=np.asarray(norm_ffn1), norm_mix=np.asarray(norm_mix), norm_ffn2=np.asarray(norm_ffn2),
             norm_ple=np.asarray(norm_ple), norm_final=np.asarray(norm_final), hgrn_lb_logits=np.asarray(hgrn_lb_logits),
             hgrn_norm=np.asarray(hgrn_norm), conv_norm=np.asarray(conv_norm), conv_w=np.asarray(conv_w))
    W = {"ffn1_gate": np.asarray(ffn1_gate)[0], "ffn1_up": np.asarray(ffn1_up)[0], "ffn1_down": np.asarray(ffn1_down)[0],
         "ffn2_gate": np.asarray(ffn2_gate)[0], "ffn2_up": np.asarray(ffn2_up)[0], "ffn2_down": np.asarray(ffn2_down)[0],
         "w_in": np.asarray(w_in)[0], "w_out": np.asarray(w_out)[0], "w_ple": np.asarray(w_ple)[0],
         "w_ple_gate": np.asarray(w_ple_gate)[0]}
    b = _get_program(_debug)
    wts = build_slabs(b.plan[b.slab_off:b.nslab], W)
    cst = _consts()
    x = np.asarray(x, np.float32)
    p = np.asarray(p, np.float32)
    in_maps = []
    for c in range(NCORES):
        bi, half = divmod(c, 2)
        xs = x[bi, half * T:(half + 1) * T, :]
        xT = np.ascontiguousarray(xs.reshape(T, 16, 128).transpose(2, 1, 0)).reshape(128, 16 * T)
        ps_ = p[0, bi, half * T:(half + 1) * T, :]
        pT = np.ascontiguousarray(ps_.reshape(T, 2, 128).transpose(2, 1, 0)).reshape(128, 2 * T)
        in_maps.append({"xT": xT, "pT": pT, "prm": _prm(I, c), "cst": cst, "wts": wts})
    res = run_bass_kernel_spmd(b.nc, in_maps, core_ids=list(range(NCORES)))
    out = np.empty((4, 2 * T, D), np.float32)
    for c in range(NCORES):
        bi, half = divmod(c, 2)
        yT = res.results[c]["yT"].reshape(128, 16, T)
        out[bi, half * T:(half + 1) * T, :] = yT.transpose(2, 1, 0).reshape(T, D)
    if _debug:
        return out, res.results
    return out
```

```python
import contextlib
import numpy as np
import concourse.bass as bass
import concourse.mybir as mybir
from concourse.bass_utils import run_bass_kernel_spmd

F32 = mybir.dt.float32
BF16 = mybir.dt.bfloat16
AF = mybir.ActivationFunctionType
ALU = mybir.AluOpType

ENGS = ("pe", "act", "dve", "pool", "sp")

D = 2048
T = 1024
TH = 512
NCH = 16
DFF = 5632
NFC = 44
GRP = 4
NGRP = 11
EPS = 1e-6
NCORES = 8


class Op:
    __slots__ = ("eng", "fn", "deps", "needs_inc", "inc_val", "dma_key", "dma_val", "dma_inc", "name")

    def __init__(self, eng, fn, name=""):
        self.eng = eng
        self.fn = fn
        self.deps = []
        self.needs_inc = False
        self.inc_val = None
        self.dma_key = None
        self.dma_val = None
        self.dma_inc = 16
        self.name = name


class Reg:
    __slots__ = ("space", "lo", "hi")

    def __init__(self, space, lo, hi):
        self.space = space
        self.lo = lo
        self.hi = hi


class Sched:
    def __init__(self):
        self.ops = {e: [] for e in ENGS}
        self.ent = {}
        self.dma_tot = {}
        self.dma_group = set()

    def _overl(self, reg):
        lst = self.ent.setdefault(reg.space, [])
        return [e for e in lst if e[0] < reg.hi and reg.lo < e[1]]

    def add(self, eng, fn, reads=(), writes=(), dma_key=None, dma_inc=16, name=""):
        op = Op(eng, fn, name)
        deps = {}
        for r in reads:
            for e in self._overl(r):
                if e[2] is not None:
                    deps[id(e[2])] = (e[2], "raw")
        for w in writes:
            for e in self._overl(w):
                if e[2] is not None and id(e[2]) not in deps:
                    deps[id(e[2])] = (e[2], "waw")
                for rd in e[3]:
                    if id(rd) not in deps:
                        deps[id(rd)] = (rd, "war")
        for d, kind in deps.values():
            if d is op:
                continue
            if d.dma_key is None and dma_key is None and d.eng == eng:
                if eng == "pe":
                    continue
            op.deps.append(d)
            d.needs_inc = True
        for w in writes:
            lst = self.ent.setdefault(w.space, [])
            keep = [e for e in lst if not (w.lo <= e[0] and e[1] <= w.hi)]
            keep.append([w.lo, w.hi, op, []])
            self.ent[w.space] = keep
        for r in reads:
            found = False
            for e in self._overl(r):
                e[3].append(op)
                if e[0] <= r.lo and r.hi <= e[1]:
                    found = True
            if not found:
                self.ent.setdefault(r.space, []).append([r.lo, r.hi, None, [op]])
        if dma_key is not None:
            op.dma_key = dma_key
            op.dma_inc = dma_inc
            self.dma_tot[dma_key] = self.dma_tot.get(dma_key, 0) + dma_inc
            op.dma_val = self.dma_tot[dma_key]
        self.ops[eng].append(op)
        return op

    def emit(self, block_engines, sems, dma_sems):
        for e in ENGS:
            n = 0
            for op in self.ops[e]:
                if op.dma_key is None and op.needs_inc:
                    n += 1
                    op.inc_val = n
        for key in self.dma_group:
            for e in ENGS:
                for op in self.ops[e]:
                    if op.dma_key == key:
                        op.dma_val = self.dma_tot[key]
        stats = {}

        def body_for(e):
            def body(eng):
                seen = {}
                nw = 0
                for op in self.ops[e]:
                    need = {}
                    for d in op.deps:
                        if d.dma_key is not None:
                            k, v = ("dma", d.dma_key), d.dma_val
                        else:
                            k, v = ("eng", d.eng), d.inc_val
                        if v > need.get(k, 0):
                            need[k] = v
                    for k, v in need.items():
                        if seen.get(k, 0) >= v:
                            continue
                        seen[k] = v
                        eng.wait_ge(dma_sems[k[1]] if k[0] == "dma" else sems[k[1]], v)
                        nw += 1
                    ins = op.fn(eng)
                    if ins is None:
                        continue
                    if op.dma_key is not None:
                        ins.then_inc(dma_sems[op.dma_key], op.dma_inc)
                    elif op.needs_inc:
                        ins.then_inc(sems[e], 1)
                stats[e] = (len(self.ops[e]), nw)
            return body

        for e in ENGS:
            if self.ops[e]:
                block_engines[e](body_for(e))
        return stats


class V:
    __slots__ = ("ap", "reg")

    def __init__(self, ap, reg):
        self.ap = ap
        self.reg = reg


def slab_plan():
    plan = []

    def ffn(tag):
        def gu(g):
            for fl in range(GRP):
                f = g * GRP + fl
                plan.append((tag + "_gate", "k", f))
                plan.append((tag + "_up", "k", f))

        def dn(g):
            for fl in range(GRP):
                plan.append((tag + "_down", "d", g * GRP + fl))
        gu(0)
        for g in range(1, NGRP):
            gu(g)
            dn(g - 1)
        dn(NGRP - 1)

    ffn("ffn1")
    mix_start = len(plan)
    for hd in range(8):
        plan.append(("w_in", "k", 0 + hd))
        plan.append(("w_in", "k", 8 + hd))
        plan.append(("w_in", "k", 16 + hd))
    for c in range(8):
        plan.append(("w_in", "k", 32 + c))
        plan.append(("w_in", "k", 40 + c))
        plan.append(("w_in", "k", 48 + c))
    for hd in range(8):
        plan.append(("w_in", "k", 24 + hd))
    for n in range(16):
        plan.append(("w_out", "k", n))
    mix_end = len(plan)
    ffn("ffn2")
    plan.append(("w_ple", "d", 0))
    plan.append(("w_ple", "d", 1))
    for n in range(16):
        plan.append(("w_ple_gate", "k", n))
    return plan, mix_start, mix_end


def build_slabs(plan, W):
    out = np.empty((len(plan), 128, 2048), np.float32)
    for i, (name, kind, idx) in enumerate(plan):
        w = W[name]
        if kind == "k":
            blk = w[:, idx * 128:(idx + 1) * 128]
            out[i] = blk.reshape(16, 128, 128).transpose(1, 0, 2).reshape(128, 2048)
        else:
            out[i] = w[idx * 128:(idx + 1) * 128, :]
    return out


PC_G1, PC_GM, PC_G2, PC_GP, PC_GF = 0, 16, 32, 48, 64
PC_L0, PC_L1 = 80, 88
PC_HGN = 96
PC_CVN = 97
PC_CW = 105
PC_FLAG = 129
PC_EPS = 130
NPRM_IN = 131
PC_LB, PC_OML, PC_NOML = 131, 139, 147
NPRM = 160
NCST = 512

A_PRM = 0
A_CSTB = 160
A_H = 416
A_X = A_H + 16384
A_RING = A_X + 8192
NSLOT = 12
NSLOT_MIX = 6
A_MIX = A_RING + NSLOT_MIX * 1024
A_PHASE = A_RING + NSLOT * 1024
A_END = 52000


class Builder:
    def __init__(self, debug=False, stop=None, nslab=None, skip1=False):
        self.debug = debug
        self.stop = stop
        self.skip1 = skip1
        self.used_slabs = None
        self.taps = []
        nc = bass.Bass("TRN2", target_bir_lowering=False)
        self.nc = nc
        self.S = Sched()
        self.plan, self.mix_start, self.mix_end = slab_plan()
        ns = len(self.plan) if nslab is None else nslab
        self.nslab = ns
        self.slab_off = self.mix_start if skip1 else 0
        self.d_x = nc.dram_tensor("xT", [128, 16 * T], F32, kind="ExternalInput")
        self.d_p = nc.dram_tensor("pT", [128, 2 * T], F32, kind="ExternalInput")
        self.d_prm = nc.dram_tensor("prm", [128, NPRM_IN], F32, kind="ExternalInput")
        self.d_cst = nc.dram_tensor("cst", [128, NCST], F32, kind="ExternalInput")
        self.d_w = nc.dram_tensor("wts", [ns - self.slab_off, 128, 2048], F32, kind="ExternalInput")
        self.d_y = nc.dram_tensor("yT", [128, 16 * T], F32, kind="ExternalOutput")
        self.d_s1 = nc.dram_tensor("send1", [128, 1024], F32)
        self.d_r1 = nc.dram_tensor("recv1", [256, 1024], F32)
        self.d_s2 = nc.dram_tensor("send2", [128, 16], F32)
        self.d_r2 = nc.dram_tensor("recv2", [256, 16], F32)
        self.stack = contextlib.ExitStack()
        self.sb = self.stack.enter_context(nc.sbuf_tensor("arena", [128, A_END], F32))
        self.ps = self.stack.enter_context(nc.psum_tensor("psum", [128, 4096], F32))
        self.cnt = 0

    def f32(self, lo, n):
        return V(self.sb[:, lo:lo + n], Reg("sb", lo, lo + n))

    def bf(self, lo, nwords):
        return V(self.sb[:, lo:lo + nwords].bitcast(BF16), Reg("sb", lo, lo + nwords))

    def bank(self, b):
        return V(self.ps[:, b * 512:(b + 1) * 512], Reg("ps", b * 512, (b + 1) * 512))

    def prm(self, col, n=1):
        return V(self.sb[:, A_PRM + col:A_PRM + col + n], Reg("sb", A_PRM + col, A_PRM + col + n))

    def op(self, eng, fn, reads=(), writes=(), **kw):
        return self.S.add(eng, fn, reads=[r.reg if isinstance(r, V) else r for r in reads],
                          writes=[w.reg if isinstance(w, V) else w for w in writes], **kw)

    def tap(self, name, v, ncols, dt=F32):
        if not self.debug:
            return
        d = self.nc.dram_tensor("dbg_" + name, [128, ncols], dt, kind="ExternalOutput")
        self.taps.append("dbg_" + name)
        step = 1024
        for c0 in range(0, ncols, step):
            c1 = min(ncols, c0 + step)
            self.op("sp", lambda e, c0=c0, c1=c1: e.dma_start(out=d[:, c0:c1], in_=v.ap[:, c0:c1]), reads=[v], dma_key="tap_" + name)
        self.S.dma_group.add("tap_" + name)

    def ring_init(self):
        self.r_free = list(range(NSLOT))
        self.r_next = self.slab_off
        self.r_hold = True
        self.r_slot = {}
        self.ring_fill()

    def slot_view(self, s):
        return self.bf(A_RING + s * 1024, 1024)

    def ring_fill(self):
        while self.r_next < self.nslab:
            i = self.r_next
            mix = self.mix_start <= i < self.mix_end
            cand = [s for s in self.r_free if (not mix) or s < NSLOT_MIX]
            if not cand:
                return
            s = cand[0]
            self.r_free.remove(s)
            self.r_slot[i] = s
            v = self.slot_view(s)
            src = self.d_w[i - self.slab_off]
            rd = [self.f32(A_H, 16384)] if (self.r_hold and i < self.slab_off + 3) else []
            self.op("pool", lambda e, v=v, src=src: e.dma_start(out=v.ap, in_=src), reads=rd, writes=[v], dma_key="r%d" % s)
            self.r_next += 1

    def ring_get(self, i):
        assert i in self.r_slot, "ring too small / order violated at slab %d (%s)" % (i, self.plan[i])
        return self.slot_view(self.r_slot[i])

    def ring_rel(self, i):
        self.r_free.append(self.r_slot.pop(i))
        self.r_free.sort()
        self.ring_fill()

    def mm(self, out_ap, out_v, lhsT_ap, rhs_ap, reads, start, stop):
        self.op("pe", lambda e: e.matmul(out_ap, lhsT=lhsT_ap, rhs=rhs_ap, start=start, stop=stop),
                reads=reads, writes=[out_v])

    def rmsnorm_sums(self, src_chunks, nparts_scale, sq_bufs, bsum):
        n = len(src_chunks)
        for c, sv in enumerate(src_chunks):
            sq = sq_bufs[c % len(sq_bufs)]
            if c % 2 == 0:
                self.op("act", lambda e, sq=sq, sv=sv: e.activation(out=sq.ap, in_=sv.ap, func=AF.Square), reads=[sv], writes=[sq])
            else:
                self.op("dve", lambda e, sq=sq, sv=sv: e.tensor_tensor(out=sq.ap, in0=sv.ap, in1=sv.ap, op=ALU.mult), reads=[sv], writes=[sq])
            for th in range(2):
                b = bsum[th]
                self.mm(b.ap, b, self.ONESB.ap, sq.ap[:, th * TH:(th + 1) * TH], [self.ONESB, sq], c == 0, c == n - 1)

    def norm_to_X(self, gcol):
        R = self.f32(self.a_R, 1024)
        sqs = [self.bf(self.a_SQ + i * 512, 512) for i in range(2)] + [self.bf(self.a_R + 1024 + i * 512, 512) for i in range(2)]
        bs = [self.bank(6), self.bank(7)]
        self.rmsnorm_sums([self.Hc(c) for c in range(16)], 1.0 / D, sqs, bs)
        eps = self.prm(PC_EPS)
        for th in range(2):
            self.op("act", lambda e, th=th: e.activation(out=R.ap[:, th * TH:(th + 1) * TH], in_=bs[th].ap, func=AF.Ln,
                                                         scale=1.0 / D, bias=eps.ap), reads=[bs[th], eps], writes=[R])
        self.op("act", lambda e: e.activation(out=R.ap, in_=R.ap, func=AF.Exp, scale=-0.5), reads=[R], writes=[R])
        for c in range(16):
            g = self.prm(gcol + c)
            xc = self.Xc(c)
            hc = self.Hc(c)
            self.op("dve", lambda e, g=g, xc=xc, hc=hc: e.scalar_tensor_tensor(
                out=xc.ap, in0=hc.ap, scalar=g.ap, in1=R.ap, op0=ALU.mult, op1=ALU.mult), reads=[hc, g, R], writes=[xc])

    def Hc(self, c):
        return self.f32(A_H + c * 1024, 1024)

    def Xc(self, c):
        return self.bf(A_X + c * 512, 512)

    def ffn(self, base, gcol):
        self.a_R = A_PHASE + 6144
        self.a_SQ = A_PHASE + 5120
        self.norm_to_X(gcol)
        HT = [self.bf(A_PHASE + i * 2048, 2048) for i in range(2)]
        T1 = [self.f32(A_PHASE + 4096 + i * 512, 512) for i in range(2)]
        st = {"gu": 0, "dn": 0, "slab": base}

        def take():
            i = st["slab"]
            st["slab"] += 1
            return i

        def gu(g):
            ht = HT[g % 2]
            for fl in range(GRP):
                ig, iu = take(), take()
                sg, su = self.ring_get(ig), self.ring_get(iu)
                sg3 = sg.ap.rearrange("p (k n) -> p k n", n=128)
                su3 = su.ap.rearrange("p (k n) -> p k n", n=128)
                for th in range(2):
                    u = st["gu"]
                    st["gu"] += 1
                    bg, bu = self.bank(2 * (u % 3)), self.bank(2 * (u % 3) + 1)
                    for kc in range(16):
                        xk = self.Xc(kc)
                        self.mm(bg.ap, bg, sg3[:, kc, :], xk.ap[:, th * TH:(th + 1) * TH], [sg, xk], kc == 0, kc == 15)
                    for kc in range(16):
                        xk = self.Xc(kc)
                        self.mm(bu.ap, bu, su3[:, kc, :], xk.ap[:, th * TH:(th + 1) * TH], [su, xk], kc == 0, kc == 15)
                    t1 = T1[u % 2]
                    self.op("act", lambda e, t1=t1, bg=bg: e.activation(out=t1.ap, in_=bg.ap, func=AF.Silu), reads=[bg], writes=[t1])
                    lo = fl * 1024 + th * TH
                    hv = V(ht.ap[:, lo:lo + TH], Reg("sb", ht.reg.lo + lo // 2, ht.reg.lo + (lo + TH) // 2))
                    self.op("dve", lambda e, hv=hv, t1=t1, bu=bu: e.tensor_tensor(out=hv.ap, in0=t1.ap, in1=bu.ap, op=ALU.mult),
                            reads=[t1, bu], writes=[hv])
                self.ring_rel(ig)
                self.ring_rel(iu)

        def dn(g):
            ht = HT[g % 2]
            ids = [take() for _ in range(GRP)]
            sl = [self.ring_get(i) for i in ids]
            for n in range(16):
                for th in range(2):
                    b = self.bank((6 + st["dn"]) % 8)
                    st["dn"] += 1
                    for fl in range(GRP):
                        lo = fl * 1024 + th * TH
                        self.mm(b.ap, b, sl[fl].ap[:, n * 128:(n + 1) * 128], ht.ap[:, lo:lo + TH], [sl[fl], ht], fl == 0, fl == GRP - 1)
                    hv = V(self.Hc(n).ap[:, th * TH:(th + 1) * TH], Reg("sb", A_H + n * 1024 + th * TH, A_H + n * 1024 + (th + 1) * TH))
                    self.op("dve", lambda e, hv=hv, b=b: e.scalar_tensor_tensor(
                        out=hv.ap, in0=b.ap, scalar=0.5, in1=hv.ap, op0=ALU.mult, op1=ALU.add), reads=[b, hv], writes=[hv])
            for i in ids:
                self.ring_rel(i)

        gu(0)
        for g in range(1, NGRP):
            gu(g)
            dn(g - 1)
        dn(NGRP - 1)
        return st["slab"]

    def build(self):
        S = self.S
        PRM = self.f32(A_PRM, NPRM_IN)
        self.op("sp", lambda e: e.dma_start(out=PRM.ap, in_=self.d_prm[:, :]), writes=[PRM], dma_key="ld")
        CSTB = self.bf(A_CSTB, 256)
        self.op("pool", lambda e: e.dma_start(out=CSTB.ap, in_=self.d_cst[:, :]), writes=[CSTB], dma_key="ldc")
        self.IDB = V(CSTB.ap[:, 0:128], CSTB.reg)
        self.ONESB = V(CSTB.ap[:, 128:256], CSTB.reg)
        self.MASK = V(CSTB.ap[:, 256:512], CSTB.reg)
        for c in range(16):
            hc = self.Hc(c)
            self.op("sp", lambda e, hc=hc, c=c: e.dma_start(out=hc.ap, in_=self.d_x[:, c * T:(c + 1) * T]), writes=[hc], dma_key="lx%d" % c)
        S.dma_group.add("ld")
        self.ring_init()
        LB, OML, NOML = self.prm(PC_LB, 8), self.prm(PC_OML, 8), self.prm(PC_NOML, 8)
        L0, L1 = self.prm(PC_L0, 8), self.prm(PC_L1, 8)
        self.op("dve", lambda e: e.tensor_tensor(out=LB.ap, in0=L0.ap, in1=L1.ap, op=ALU.subtract), reads=[L0, L1], writes=[LB])
        self.op("act", lambda e: e.activation(out=LB.ap, in_=LB.ap, func=AF.Sigmoid), reads=[LB], writes=[LB])
        self.op("dve", lambda e: e.tensor_scalar(out=OML.ap, in0=LB.ap, scalar1=-1.0, scalar2=1.0, op0=ALU.mult, op1=ALU.add), reads=[LB], writes=[OML])
        self.op("dve", lambda e: e.tensor_scalar(out=NOML.ap, in0=OML.ap, scalar1=-1.0, scalar2=None, op0=ALU.mult), reads=[OML], writes=[NOML])

        if self.skip1:
            nxt = self.mix_start
        else:
            nxt = self.ffn(0, PC_G1)
        self.tap("h1", self.f32(A_H, 16384), 16384)
        if self.stop == "ffn1":
            self.used_slabs = nxt
            return self.final()
        nxt = self.mixer(nxt)
        self.tap("h2", self.f32(A_H, 16384), 16384)
        if self.stop is not None and self.stop != "ffn2":
            self.used_slabs = max(nxt, self.r_next)
            return self.final()
        PT = self.bf(A_PHASE + 8192, 1024)
        self.op("pool", lambda e: e.dma_start(out=PT.ap, in_=self.d_p[:, :]), writes=[PT], dma_key="ldp")
        nxt = self.ffn(nxt, PC_G2)
        self.tap("h3", self.f32(A_H, 16384), 16384)
        if self.stop == "ffn2":
            self.used_slabs = nxt
            return self.final()
        nxt = self.ple(nxt, PT)
        assert nxt == len(self.plan)
        self.final()

    def ple(self, base, PT):
        self.norm_to_X(PC_GP)
        i0, i1 = base, base + 1
        sp = [self.ring_get(i0), self.ring_get(i1)]
        T1 = [self.f32(A_PHASE + 4096 + i * 512, 512) for i in range(2)]
        u = 0
        for n in range(16):
            i = base + 2 + n
            s = self.ring_get(i)
            s3 = s.ap.rearrange("p (k n) -> p k n", n=128)
            for th in range(2):
                ba, bb = self.bank(2 * (u % 3)), self.bank(2 * (u % 3) + 1)
                for kc in range(16):
                    xk = self.Xc(kc)
                    self.mm(ba.ap, ba, s3[:, kc, :], xk.ap[:, th * TH:(th + 1) * TH], [s, xk], kc == 0, kc == 15)
                for kc in range(2):
                    self.mm(bb.ap, bb, sp[kc].ap[:, n * 128:(n + 1) * 128], PT.ap[:, kc * T + th * TH:kc * T + (th + 1) * TH],
                            [sp[kc], PT], kc == 0, kc == 1)
                t1 = T1[u % 2]
                self.op("act", lambda e, t1=t1, ba=ba: e.activation(out=t1.ap, in_=ba.ap, func=AF.Sigmoid), reads=[ba], writes=[t1])
                self.op("dve", lambda e, t1=t1, bb=bb: e.tensor_tensor(out=t1.ap, in0=t1.ap, in1=bb.ap, op=ALU.mult), reads=[t1, bb], writes=[t1])
                hv = V(self.Hc(n).ap[:, th * TH:(th + 1) * TH], Reg("sb", A_H + n * 1024 + th * TH, A_H + n * 1024 + (th + 1) * TH))
                self.op("dve", lambda e, hv=hv, t1=t1: e.tensor_tensor(out=hv.ap, in0=hv.ap, in1=t1.ap, op=ALU.add), reads=[hv, t1], writes=[hv])
                u += 1
            self.ring_rel(i)
        self.ring_rel(i0)
        self.ring_rel(i1)
        return base + 18

    def final(self):
        R = self.f32(self.a_R, 1024)
        sqs = [self.bf(self.a_SQ + i * 512, 512) for i in range(2)]
        bs = [self.bank(6), self.bank(7)]
        self.rmsnorm_sums([self.Hc(c) for c in range(16)], 1.0 / D, sqs, bs)
        eps = self.prm(PC_EPS)
        for th in range(2):
            self.op("act", lambda e, th=th: e.activation(out=R.ap[:, th * TH:(th + 1) * TH], in_=bs[th].ap, func=AF.Ln,
                                                         scale=1.0 / D, bias=eps.ap), reads=[bs[th], eps], writes=[R])
        self.op("act", lambda e: e.activation(out=R.ap, in_=R.ap, func=AF.Exp, scale=-0.5), reads=[R], writes=[R])
        OB = [self.f32(A_PHASE + i * 1024, 1024) for i in range(4)]
        outs = []
        for c in range(16):
            g = self.prm(PC_GF + c)
            ob = OB[c % 4]
            hc = self.Hc(c)
            self.op("dve", lambda e, g=g, ob=ob, hc=hc: e.scalar_tensor_tensor(
                out=ob.ap, in0=hc.ap, scalar=g.ap, in1=R.ap, op0=ALU.mult, op1=ALU.mult), reads=[hc, g, R], writes=[ob])
            outs.append(self.op("sp", lambda e, ob=ob, c=c: e.dma_start(out=self.d_y[:, c * T:(c + 1) * T], in_=ob.ap),
                                reads=[ob], dma_key="out%d" % (c % 4)))
        fin = self.S.add("sp", lambda e: None, name="final")
        for e in ENGS:
            for op in self.S.ops[e]:
                if op.dma_key is not None and (op.dma_key.startswith("out") or op.dma_key.startswith("tap")):
                    fin.deps.append(op)

    def mixer(self, base):
        self.a_R = A_PHASE + 6144
        self.a_SQ = A_PHASE + 5120
        self.norm_to_X(PC_GM)
        a = A_MIX
        a_OLOC = a
        a += 8192
        a_QH = a
        a += 4096
        a_Y = a
        a += 4096
        a_SCR = a
        a_HSCR = a_Y
        assert A_END - a_SCR >= 4200

        def OLOC(hd, th):
            lo = a_OLOC + hd * 1024 + th * TH
            return self.f32(lo, TH)

        def QH(hd, th=None):
            if th is None:
                return self.bf(a_QH + hd * 512, 512)
            return self.bf(a_QH + hd * 512 + th * 256, 256)

        def Yv(c, th=None):
            if th is None:
                return self.bf(a_Y + c * 512, 512)
            return self.bf(a_Y + c * 512 + th * 256, 256)

        slab = {"i": base}

        def take():
            i = slab["i"]
            slab["i"] += 1
            return i

        p = a_HSCR
        sets = []
        for s in range(2):
            d = {}
            d["A2"] = self.f32(p, 512); p += 512
            d["A3"] = self.f32(p, 512); p += 512
            d["A4"] = self.f32(p, 512); p += 512
            d["QT"] = self.bf(p, 256); p += 256
            d["KT"] = self.bf(p, 256); p += 256
            d["KTM"] = self.bf(p, 256); p += 256
            d["VTM"] = self.bf(p, 256); p += 256
            d["S1B"] = self.bf(p, 512); p += 512
            d["PSB"] = self.bf(p, 256); p += 256
            d["DD"] = self.f32(p, 24); p += 24
            d["EE"] = self.f32(p, 24); p += 24
            sets.append(d)
        SR = []
        for i in range(4):
            SR.append(self.f32(p, 128)); p += 128
        SEND = self.f32(p, 1024); p += 1024
        assert p <= A_END, p
        hslabs = {}

        def hg_P(u, part=2):
            hd, th = divmod(u, 2)
            s = u % 2
            if th == 0 and part != 1:
                hslabs[hd] = (take(), take(), take())
            iq, if_, ii = hslabs[hd]
            sq, sf, si = self.ring_get(iq), self.ring_get(if_), self.ring_get(ii)
            sq3 = sq.ap.rearrange("p (k n) -> p k n", n=128)
            sf3 = sf.ap.rearrange("p (k n) -> p k n", n=128)
            si3 = si.ap.rearrange("p (k n) -> p k n", n=128)
            bq, bf_, bv = self.bank(4 * s), self.bank(4 * s + 1), self.bank(4 * s + 2)
            if part != 1:
                for kc in range(16):
                    xk = self.Xc(kc)
                    self.mm(bq.ap, bq, sq3[:, kc, :], xk.ap[:, th * TH:(th + 1) * TH], [sq, xk], kc == 0, kc == 15)
                for kc in range(16):
                    xk = self.Xc(kc)
                    self.mm(bf_.ap, bf_, sf3[:, kc, :], xk.ap[:, th * TH:(th + 1) * TH], [sf, xk], kc == 0, kc == 15)
            if part == 0:
                return
            for tt in range(4):
                t0 = th * TH + tt * 128
                for kc in range(16):
                    xk = self.Xc(kc)
                    self.mm(bv.ap[:, tt * 128:(tt + 1) * 128], bv, xk.ap[:, t0:t0 + 128], si3[:, kc, :], [si, xk], kc == 0, kc == 15)
            if th == 1:
                for i in hslabs[hd]:
                    self.ring_rel(i)

        def hg_E(u):
            hd, th = divmod(u, 2)
            s = u % 2
            d = sets[s]
            dp = sets[1 - s]
            bq, bf_, bv = self.bank(4 * s), self.bank(4 * s + 1), self.bank(4 * s + 2)
            A2, A3, A4, QT, KT, VTM, DD, EE = d["A2"], d["A3"], d["A4"], d["QT"], d["KT"], d["VTM"], d["DD"], d["EE"]
            lb, oml, noml = self.prm(PC_LB + hd), self.prm(PC_OML + hd), self.prm(PC_NOML + hd)
            op = self.op
            op("act", lambda e: e.activation(out=bf_.ap, in_=bf_.ap, func=AF.Sigmoid), reads=[bf_], writes=[bf_])
            op("act", lambda e: e.activation(out=A4.ap, in_=bq.ap, func=AF.Sigmoid), reads=[bq], writes=[A4])
            op("act", lambda e: e.activation(out=A2.ap, in_=bf_.ap, func=AF.Ln, scale=oml.ap, bias=lb.ap), reads=[bf_, oml, lb], writes=[A2])
            op("dve", lambda e: e.tensor_tensor(out=bq.ap, in0=bq.ap, in1=A4.ap, op=ALU.mult), reads=[bq, A4], writes=[bq])
            op("dve", lambda e: e.tensor_scalar(out=bf_.ap, in0=bf_.ap, scalar1=noml.ap, scalar2=oml.ap, op0=ALU.mult, op1=ALU.add),
               reads=[bf_, noml, oml], writes=[bf_])
            if th == 0:
                op("dve", lambda e: e.tensor_tensor_scan(out=A3.ap, data0=A2.ap, data1=A2.ap, initial=0.0, op0=ALU.add, op1=ALU.bypass),
                   reads=[A2], writes=[A3])
            else:
                pa3 = dp["A3"]
                op("dve", lambda e: e.tensor_tensor_scan(out=A3.ap, data0=A2.ap, data1=A2.ap, initial=pa3.ap[:, 511:512], op0=ALU.add, op1=ALU.bypass),
                   reads=[A2, pa3], writes=[A3])
            A3v = A3.ap.rearrange("p (c t) -> p c t", t=64)
            A2v = A2.ap.rearrange("p (c t) -> p c t", t=64)
            op("dve", lambda e: e.tensor_tensor(out=A2v, in0=A3v, in1=A3v[:, :, 31:32].broadcast_to([128, 8, 64]), op=ALU.subtract),
               reads=[A3], writes=[A2])
            op("act", lambda e: e.activation(out=A4.ap, in_=A2.ap, func=AF.Exp), reads=[A2], writes=[A4])
            op("dve", lambda e: e.tensor_tensor(out=QT.ap, in0=bq.ap, in1=A4.ap, op=ALU.mult), reads=[bq, A4], writes=[QT])
            op("act", lambda e: e.activation(out=A4.ap, in_=A2.ap, func=AF.Exp, scale=-1.0), reads=[A2], writes=[A4])
            op("dve", lambda e: e.tensor_tensor(out=KT.ap, in0=bf_.ap, in1=A4.ap, op=ALU.mult), reads=[bf_, A4], writes=[KT])
            op("act", lambda e: e.activation(out=A4.ap, in_=A3.ap, func=AF.Exp), reads=[A3], writes=[A4])
            qh = QH(hd, th)
            op("dve", lambda e: e.tensor_tensor(out=qh.ap, in0=bq.ap, in1=A4.ap, op=ALU.mult), reads=[bq, A4], writes=[qh])
            DDv = DD.ap.rearrange("p (j c) -> p j c", c=8)
            Bref = A3v[:, :, 31]
            Blast = A3v[:, :, 63]
            if th == 0:
                op("dve", lambda e: e.tensor_copy(out=DDv[:, 0, 0:1], in_=Bref[:, 0:1]), reads=[A3], writes=[DD])
                op("dve", lambda e: e.tensor_copy(out=DDv[:, 2, 0:1], in_=Blast[:, 0:1]), reads=[A3], writes=[DD])
            else:
                pa3 = dp["A3"]
                op("dve", lambda e: e.tensor_tensor(out=DDv[:, 0, 0:1], in0=Bref[:, 0:1], in1=pa3.ap[:, 511:512], op=ALU.subtract), reads=[A3, pa3], writes=[DD])
                op("dve", lambda e: e.tensor_tensor(out=DDv[:, 2, 0:1], in0=Blast[:, 0:1], in1=pa3.ap[:, 511:512], op=ALU.subtract), reads=[A3, pa3], writes=[DD])
            op("dve", lambda e: e.tensor_tensor(out=DDv[:, 0, 1:8], in0=Bref[:, 1:8], in1=Blast[:, 0:7], op=ALU.subtract), reads=[A3], writes=[DD])
            op("dve", lambda e: e.tensor_tensor(out=DDv[:, 1, :], in0=Blast, in1=Bref, op=ALU.subtract), reads=[A3], writes=[DD])
            op("dve", lambda e: e.tensor_tensor(out=DDv[:, 2, 1:8], in0=Blast[:, 1:8], in1=Blast[:, 0:7], op=ALU.subtract), reads=[A3], writes=[DD])
            op("act", lambda e: e.activation(out=EE.ap, in_=DD.ap, func=AF.Exp), reads=[DD], writes=[EE])
            op("act", lambda e: e.activation(out=VTM.ap, in_=bv.ap, func=AF.Copy), reads=[bv], writes=[VTM])

        def hg_small(u, part=0):
            hd, th = divmod(u, 2)
            s = u % 2
            d = sets[s]
            bq, bf_, bv, bm = self.bank(4 * s), self.bank(4 * s + 1), self.bank(4 * s + 2), self.bank(4 * s + 3)
            QT, KT, KTM, VTM, S1B, PSB, EE = d["QT"], d["KT"], d["KTM"], d["VTM"], d["S1B"], d["PSB"], d["EE"]
            op = self.op
            EEv = EE.ap.rearrange("p (j c) -> p j c", c=8)
            bmb = bm.ap.bitcast(BF16)
            S1v = S1B.ap.rearrange("p (c v) -> p c v", v=128)
            if part in (0, 1):
                for tt in range(4):
                    op("pe", lambda e, tt=tt: e.transpose(out=bmb[:, tt * 128:(tt + 1) * 128], in_=KT.ap[:, tt * 128:(tt + 1) * 128], identity=self.IDB.ap),
                       reads=[KT, self.IDB], writes=[bm])
                op("act", lambda e: e.activation(out=KTM.ap, in_=bmb[:, 0:512], func=AF.Copy), reads=[bm], writes=[KTM])
                if self.stop == "smA":
                    return
                for c in range(8):
                    tt, p0 = c // 2, (c % 2) * 64
                    bo = bq if c % 2 == 0 else bf_
                    op("pe", lambda e, bo=bo, c=c, tt=tt, p0=p0: e.matmul(bo.ap[:, tt * 128:(tt + 1) * 128],
                                                                        lhsT=KTM.ap[p0:p0 + 64, tt * 128:(tt + 1) * 128],
                                                                        rhs=VTM.ap[p0:p0 + 64, tt * 128:(tt + 1) * 128], start=True, stop=True),
                       reads=[KTM, VTM], writes=[bo])
                if self.stop == "smB1":
                    return
                for c in range(8):
                    p0 = (c % 2) * 64
                    col = 256 + (c // 2) * 64
                    op("pe", lambda e, c=c, p0=p0, col=col: e.matmul(bm.ap[p0:p0 + 64, col:col + 64], lhsT=KT.ap[:, c * 64:(c + 1) * 64],
                                                                    rhs=QT.ap[:, c * 64:(c + 1) * 64], start=True, stop=True),
                       reads=[KT, QT], writes=[bm])
                PSBv = PSB.ap.rearrange("p (j t) -> p j t", t=128)
                bmv = bm.ap[:, 256:512].rearrange("p (j t) -> p j t", t=64)
                if u < 2:
                    op("dve", lambda e: e.memset(PSB.ap, 0.0), writes=[PSB])
                for hf in range(2):
                    r0 = hf * 64
                    op("dve", lambda e, r0=r0: e.tensor_tensor(out=PSBv[r0:r0 + 64, :, r0:r0 + 64], in0=bmv[r0:r0 + 64, :, :],
                                                            in1=self.MASK.ap[r0:r0 + 64, 0:64].unsqueeze(1).broadcast_to([64, 4, 64]), op=ALU.mult),
                       reads=[bm, self.MASK], writes=[PSB])
                if self.stop == "smB":
                    return
                if th == 0:
                    op("dve", lambda e: e.memset(SR[3].ap, 0.0), writes=[SR[3]])
                S1v = S1B.ap.rearrange("p (c v) -> p c v", v=128)
                ELR = EEv[:, 1, :].rearrange("p (t two) -> p t two", two=2)
                for par, bo in ((0, bq), (1, bf_)):
                    bov = bo.ap.rearrange("p (t v) -> p t v", v=128)
                    op("dve", lambda e, par=par, bov=bov: e.tensor_tensor(out=bov, in0=bov, in1=ELR[:, :, par:par + 1].broadcast_to([128, 4, 128]), op=ALU.mult),
                       reads=[bo, EE], writes=[bo])
                for c in range(8):
                    bo = bq if c % 2 == 0 else bf_
                    pk = bo.ap[:, (c // 2) * 128:(c // 2 + 1) * 128]
                    prev, cur = SR[(c - 1) % 4], SR[c % 4]
                    op("act", lambda e, c=c, prev=prev: e.activation(out=S1v[:, c, :], in_=prev.ap, func=AF.Copy, scale=EEv[:, 0, c:c + 1]), reads=[prev, EE], writes=[S1B])
                    op("dve", lambda e, c=c, pk=pk, prev=prev, cur=cur: e.scalar_tensor_tensor(out=cur.ap, in0=prev.ap, scalar=EEv[:, 2, c:c + 1], in1=pk, op0=ALU.mult, op1=ALU.add),
                       reads=[prev, EE, bo], writes=[cur])
                if self.stop == "smC":
                    return
            if part in (0, 2):
                for tt in range(4):
                    op("pe", lambda e, tt=tt: e.matmul(bv.ap[:, tt * 128:(tt + 1) * 128], lhsT=VTM.ap[:, tt * 128:(tt + 1) * 128],
                                                      rhs=PSB.ap[:, tt * 128:(tt + 1) * 128], start=True, stop=False),
                       reads=[VTM, PSB], writes=[bv])
                    for c in (2 * tt, 2 * tt + 1):
                        op("pe", lambda e, c=c: e.matmul(bv.ap[:, c * 64:(c + 1) * 64], lhsT=S1v[:, c, :], rhs=QT.ap[:, c * 64:(c + 1) * 64],
                                                        start=False, stop=(c % 2 == 1)),
                           reads=[S1B, QT], writes=[bv])
                ol = OLOC(hd, th)
                op("act", lambda e: e.activation(out=ol.ap, in_=bv.ap, func=AF.Copy), reads=[bv], writes=[ol])
                if th == 1:
                    sv = V(SEND.ap[:, hd * 128:(hd + 1) * 128], Reg("sb", SEND.reg.lo + hd * 128, SEND.reg.lo + (hd + 1) * 128))
                    op("dve", lambda e: e.tensor_copy(out=sv.ap, in_=SR[3].ap), reads=[SR[3]], writes=[sv])

        hg_P(0)
        if self.stop == "hgP0":
            for i in hslabs[0]:
                self.ring_rel(i)
            return slab["i"]
        hg_E(0)
        if self.stop == "hgE0":
            self.tap("QT0", sets[0]["QT"], 512, BF16)
            self.tap("KT0", sets[0]["KT"], 512, BF16)
            self.tap("A30", sets[0]["A3"], 512)
            self.tap("EE0", sets[0]["EE"], 24)
            self.tap("VTM0", sets[0]["VTM"], 512, BF16)
            for i in hslabs[0]:
                self.ring_rel(i)
            return slab["i"]
        nun = 16
        if self.stop is not None and self.stop.startswith("hgU"):
            nun = int(self.stop[3:])
        if self.stop in ("smA", "smB", "smB1", "smC", "smD"):
            hg_small(0)
            self.tap("KTM0", sets[0]["KTM"], 512, BF16)
            self.tap("PSB0", sets[0]["PSB"], 256, BF16)
            self.tap("S1B0", sets[0]["S1B"], 1024, BF16)
            self.tap("Sst", SR[3], 128)
            self.tap("oloc", self.f32(a_OLOC, 8192), 8192)
            for i in hslabs[0]:
                self.ring_rel(i)
            return slab["i"]
        for u in range(nun):
            if u + 1 < 16:
                hg_P(u + 1, 0)
            hg_small(u, 1)
            if u + 1 < 16:
                hg_P(u + 1, 1)
            hg_small(u, 2)
            if u + 1 < 16:
                hg_E(u + 1)
        if nun < 16:
            self.tap("oloc", self.f32(a_OLOC, 8192), 8192)
            for hd in list(hslabs.keys()):
                if 2 * hd + 1 > nun:
                    for i in hslabs[hd]:
                        if i in self.r_slot:
                            self.ring_rel(i)
            return slab["i"]
        self.tap("oloc", self.f32(a_OLOC, 8192), 8192)
        self.tap("send", SEND, 1024)
        if self.stop == "hgrn":
            return slab["i"]
        rs1, rr1 = Reg("d_s1", 0, 1), Reg("d_r1", 0, 1)
        self.op("sp", lambda e: e.dma_start(out=self.d_s1[:, :], in_=SEND.ap), reads=[SEND], writes=[rs1], dma_key="snd1")
        self.op("pool", lambda e: e.collective_compute("AllGather", ALU.bypass, replica_groups=[[0, 1], [2, 3], [4, 5], [6, 7]],
                                                       ins=[self.d_s1.ap().opt()], outs=[self.d_r1.ap().opt()]),
                reads=[rs1], writes=[rr1], dma_key="cc1", dma_inc=1)

        if self.stop == "x1":
            RECVt = self.f32(a_SCR, 1024)
            self.op("sp", lambda e: e.dma_start(out=RECVt.ap, in_=self.d_r1[0:128, :]), reads=[rr1], writes=[RECVt], dma_key="rcv1")
            self.tap("recv", RECVt, 1024)
            return slab["i"]
        p = a_SCR
        csets = []
        for s in range(2):
            d = {}
            d["VS"] = self.f32(p, 512); p += 512
            d["U2"] = self.f32(p, 514); p += 514
            d["YA"] = self.f32(p, 512); p += 512
            d["SQ"] = self.bf(p, 256); p += 256
            csets.append(d)
        YPRE = self.f32(p, 16); p += 16
        BH = self.f32(p, 16); p += 16
        SEND2 = self.f32(p, 16); p += 16
        assert p <= A_END, p
        YPREv = YPRE.ap.rearrange("p (c t) -> p c t", t=2)
        BHv = BH.ap.rearrange("p (c t) -> p c t", t=2)
        SEND2v = SEND2.ap.rearrange("p (c t) -> p c t", t=2)
        cslabs = {}
        eps = self.prm(PC_EPS)

        def cv_P(u):
            c, th = divmod(u, 2)
            s = u % 2
            if th == 0:
                cslabs[c] = (take(), take(), take())
            ids = cslabs[c]
            for j in range(3):
                sl = self.ring_get(ids[j])
                s3 = sl.ap.rearrange("p (k n) -> p k n", n=128)
                b = self.bank(4 * s + j)
                for kc in range(16):
                    xk = self.Xc(kc)
                    self.mm(b.ap, b, s3[:, kc, :], xk.ap[:, th * TH:(th + 1) * TH], [sl, xk], kc == 0, kc == 15)
            if th == 1:
                for i in ids:
                    self.ring_rel(i)

        def cv_E(u):
            c, th = divmod(u, 2)
            s = u % 2
            d, dp = csets[s], csets[1 - s]
            bB, bC, bv, bn = self.bank(4 * s), self.bank(4 * s + 1), self.bank(4 * s + 2), self.bank(4 * s + 3)
            VS, U2, YA, SQ = d["VS"], d["U2"], d["YA"], d["SQ"]
            w0, w1, w2 = self.prm(PC_CW + c), self.prm(PC_CW + 8 + c), self.prm(PC_CW + 16 + c)
            cvn = self.prm(PC_CVN + c)
            op = self.op
            op("act", lambda e: e.activation(out=VS.ap, in_=bv.ap, func=AF.Copy), reads=[bv], writes=[VS])
            if th == 0:
                op("dve", lambda e: e.memset(U2.ap[:, 0:2], 0.0), writes=[U2])
            else:
                pu = dp["U2"]
                op("dve", lambda e: e.tensor_copy(out=U2.ap[:, 0:2], in_=pu.ap[:, 512:514]), reads=[pu], writes=[U2])
            op("dve", lambda e: e.tensor_tensor(out=U2.ap[:, 2:514], in0=bC.ap, in1=VS.ap, op=ALU.mult), reads=[bC, VS], writes=[U2])
            op("dve", lambda e: e.tensor_scalar(out=YA.ap, in0=U2.ap[:, 0:512], scalar1=w0.ap, scalar2=None, op0=ALU.mult), reads=[U2, w0], writes=[YA])
            op("dve", lambda e: e.scalar_tensor_tensor(out=YA.ap, in0=U2.ap[:, 1:513], scalar=w1.ap, in1=YA.ap, op0=ALU.mult, op1=ALU.add), reads=[U2, w1, YA], writes=[YA])
            op("dve", lambda e: e.scalar_tensor_tensor(out=YA.ap, in0=U2.ap[:, 2:514], scalar=w2.ap, in1=YA.ap, op0=ALU.mult, op1=ALU.add), reads=[U2, w2, YA], writes=[YA])
            if th == 0:
                op("dve", lambda e: e.tensor_copy(out=YPREv[:, c, :], in_=YA.ap[:, 0:2]), reads=[YA], writes=[YPRE])
                op("dve", lambda e: e.tensor_copy(out=BHv[:, c, :], in_=bB.ap[:, 0:2]), reads=[bB], writes=[BH])
            else:
                op("dve", lambda e: e.tensor_copy(out=SEND2v[:, c, :], in_=U2.ap[:, 512:514]), reads=[U2], writes=[SEND2])
            op("dve", lambda e: e.tensor_tensor(out=YA.ap, in0=YA.ap, in1=bB.ap, op=ALU.mult), reads=[YA, bB], writes=[YA])
            op("act", lambda e: e.activation(out=SQ.ap, in_=YA.ap, func=AF.Square), reads=[YA], writes=[SQ])
            self.mm(bn.ap, bn, self.ONESB.ap, SQ.ap, [self.ONESB, SQ], True, True)
            op("act", lambda e: e.activation(out=VS.ap, in_=bn.ap, func=AF.Ln, scale=1.0 / 128, bias=eps.ap), reads=[bn, eps], writes=[VS])
            op("act", lambda e: e.activation(out=VS.ap, in_=VS.ap, func=AF.Exp, scale=-0.5), reads=[VS], writes=[VS])
            yv = Yv(c, th)
            op("dve", lambda e: e.scalar_tensor_tensor(out=yv.ap, in0=YA.ap, scalar=cvn.ap, in1=VS.ap, op0=ALU.mult, op1=ALU.mult), reads=[YA, cvn, VS], writes=[yv])

        cv_P(0)
        for u in range(16):
            if u + 1 < 16:
                cv_P(u + 1)
            cv_E(u)
        if self.stop == "conv":
            return slab["i"]
        rs2, rr2 = Reg("d_s2", 0, 1), Reg("d_r2", 0, 1)
        self.op("sp", lambda e: e.dma_start(out=self.d_s2[:, :], in_=SEND2.ap), reads=[SEND2], writes=[rs2], dma_key="snd2")
        self.op("pool", lambda e: e.collective_compute("AllGather", ALU.bypass, replica_groups=[[0, 1], [2, 3], [4, 5], [6, 7]],
                                                       ins=[self.d_s2.ap().opt()], outs=[self.d_r2.ap().opt()]),
                reads=[rs2], writes=[rr2], dma_key="cc2", dma_inc=1)

        p = a_SCR
        psets = []
        for s in range(4):
            d = {}
            d["SQ"] = self.bf(p, 256); p += 256
            psets.append(d)
        SINB = self.bf(p, 512); p += 512
        RECV = self.f32(p, 1024); p += 1024
        RECV2 = self.f32(p, 16); p += 16
        UP = self.f32(p, 16); p += 16
        YF = self.f32(p, 16); p += 16
        TQ = self.f32(p, 24); p += 24
        RH = self.f32(p, 16); p += 16
        SQH = self.bf(p, 8); p += 8
        assert p <= YPRE.reg.lo, (p, YPRE.reg.lo)
        flag = self.prm(PC_FLAG)
        hgn = self.prm(PC_HGN)
        self.op("sp", lambda e: e.dma_start(out=RECV.ap, in_=self.d_r1[0:128, :]), reads=[rr1], writes=[RECV], dma_key="rcv1")
        SINv = SINB.ap.rearrange("p (h v) -> p h v", v=128)
        for hd in range(8):
            self.op("act", lambda e, hd=hd: e.activation(out=SINv[:, hd, :], in_=RECV.ap[:, hd * 128:(hd + 1) * 128], func=AF.Copy, scale=flag.ap),
                    reads=[RECV, flag], writes=[SINB])
        gsl = {}

        def po_P(u):
            hd, th = divmod(u, 2)
            s = u % 2
            if th == 0:
                gsl[hd] = take()
            sl = self.ring_get(gsl[hd])
            s3 = sl.ap.rearrange("p (k n) -> p k n", n=128)
            s = u % 4
            bg, bc = self.bank(2 * s), self.bank(2 * s + 1)
            for kc in range(16):
                xk = self.Xc(kc)
                self.mm(bg.ap, bg, s3[:, kc, :], xk.ap[:, th * TH:(th + 1) * TH], [sl, xk], kc == 0, kc == 15)
            qh = QH(hd, th)
            self.mm(bc.ap, bc, SINv[:, hd, :], qh.ap, [SINB, qh], True, True)
            if th == 1:
                self.ring_rel(gsl[hd])

        def po_E(u, part):
            hd, th = divmod(u, 2)
            s = u % 2
            s = u % 4
            d = psets[s]
            SQ = d["SQ"]
            bg, bc = self.bank(2 * s), self.bank(2 * s + 1)
            bn = bc
            R1 = bc
            ol = OLOC(hd, th)
            qh = QH(hd, th)
            op = self.op
            if part == 1:
                op("act", lambda e: e.activation(out=bg.ap, in_=bg.ap, func=AF.Silu), reads=[bg], writes=[bg])
                op("dve", lambda e: e.tensor_tensor(out=ol.ap, in0=ol.ap, in1=bc.ap, op=ALU.add), reads=[ol, bc], writes=[ol])
                op("act", lambda e: e.activation(out=SQ.ap, in_=ol.ap, func=AF.Square), reads=[ol], writes=[SQ])
                self.mm(bn.ap, bn, self.ONESB.ap, SQ.ap, [self.ONESB, SQ], True, True)
                return
            op("act", lambda e: e.activation(out=R1.ap, in_=bn.ap, func=AF.Ln, scale=1.0 / 128, bias=eps.ap), reads=[bn, eps], writes=[R1])
            op("act", lambda e: e.activation(out=R1.ap, in_=R1.ap, func=AF.Exp, scale=-0.5), reads=[R1], writes=[R1])
            op("dve", lambda e: e.scalar_tensor_tensor(out=ol.ap, in0=ol.ap, scalar=hgn.ap, in1=R1.ap, op0=ALU.mult, op1=ALU.mult), reads=[ol, hgn, R1], writes=[ol])
            op("dve", lambda e: e.tensor_tensor(out=qh.ap, in0=ol.ap, in1=bg.ap, op=ALU.mult), reads=[ol, bg], writes=[qh])

        for u in range(3):
            po_P(u)
        po_E(0, 1)
        for u in range(16):
            if u + 3 < 16:
                po_P(u + 3)
            if u + 1 < 16:
                po_E(u + 1, 1)
            po_E(u, 2)

        if self.stop == "post":
            return slab["i"]
        op = self.op
        op("sp", lambda e: e.dma_start(out=RECV2.ap, in_=self.d_r2[0:128, :]), reads=[rr2], writes=[RECV2], dma_key="rcv2")
        op("dve", lambda e: e.tensor_scalar(out=UP.ap, in0=RECV2.ap, scalar1=flag.ap, scalar2=None, op0=ALU.mult), reads=[RECV2, flag], writes=[UP])
        UPv = UP.ap.rearrange("p (c t) -> p c t", t=2)
        YFv = YF.ap.rearrange("p (c t) -> p c t", t=2)
        TQv = TQ.ap.rearrange("p (j c) -> p j c", c=8)
        CW0 = self.prm(PC_CW, 8)
        CW1 = self.prm(PC_CW + 8, 8)
        CVN = self.prm(PC_CVN, 8)
        op("dve", lambda e: e.tensor_tensor(out=TQv[:, 0, :], in0=UPv[:, :, 0], in1=CW0.ap, op=ALU.mult), reads=[UP, CW0], writes=[TQ])
        op("dve", lambda e: e.tensor_tensor(out=TQv[:, 1, :], in0=UPv[:, :, 1], in1=CW1.ap, op=ALU.mult), reads=[UP, CW1], writes=[TQ])
        op("dve", lambda e: e.tensor_tensor(out=TQv[:, 2, :], in0=UPv[:, :, 1], in1=CW0.ap, op=ALU.mult), reads=[UP, CW0], writes=[TQ])
        op("dve", lambda e: e.tensor_tensor(out=YFv[:, :, 0], in0=YPREv[:, :, 0], in1=TQv[:, 0, :], op=ALU.add), reads=[YPRE, TQ], writes=[YF])
        op("dve", lambda e: e.tensor_tensor(out=YFv[:, :, 0], in0=YFv[:, :, 0], in1=TQv[:, 1, :], op=ALU.add), reads=[YF, TQ], writes=[YF])
        op("dve", lambda e: e.tensor_tensor(out=YFv[:, :, 1], in0=YPREv[:, :, 1], in1=TQv[:, 2, :], op=ALU.add), reads=[YPRE, TQ], writes=[YF])
        op("dve", lambda e: e.tensor_tensor(out=YF.ap, in0=YF.ap, in1=BH.ap, op=ALU.mult), reads=[YF, BH], writes=[YF])
        op("act", lambda e: e.activation(out=SQH.ap, in_=YF.ap, func=AF.Square), reads=[YF], writes=[SQH])
        bh = self.bank(3)
        self.mm(bh.ap[:, 0:16], bh, self.ONESB.ap, SQH.ap, [self.ONESB, SQH], True, True)
        op("act", lambda e: e.activation(out=RH.ap, in_=bh.ap[:, 0:16], func=AF.Ln, scale=1.0 / 128, bias=eps.ap), reads=[bh, eps], writes=[RH])
        op("act", lambda e: e.activation(out=RH.ap, in_=RH.ap, func=AF.Exp, scale=-0.5), reads=[RH], writes=[RH])
        op("dve", lambda e: e.tensor_tensor(out=YF.ap, in0=YF.ap, in1=RH.ap, op=ALU.mult), reads=[YF, RH], writes=[YF])
        op("dve", lambda e: e.tensor_tensor(out=YFv, in0=YFv, in1=CVN.ap.unsqueeze(2).broadcast_to([128, 8, 2]), op=ALU.mult), reads=[YF, CVN], writes=[YF])
        Yall = self.bf(a_Y, 4096)
        Yallv = Yall.ap.rearrange("p (c t) -> p c t", t=1024)
        op("dve", lambda e: e.tensor_copy(out=Yallv[:, :, 0:2], in_=YFv), reads=[YF], writes=[Yall])
        self.tap("ycv", Yall, 8192, BF16)
        self.tap("ohg", self.bf(a_QH, 4096), 8192, BF16)

        if self.stop == "halo":
            return slab["i"]
        bi = 0
        for n in range(16):
            i = take()
            sl = self.ring_get(i)
            s3 = sl.ap.rearrange("p (k n) -> p k n", n=128)
            for th in range(2):
                b = self.bank(bi % 8)
                bi += 1
                for kc in range(16):
                    src = QH(kc, th) if kc < 8 else Yv(kc - 8, th)
                    self.mm(b.ap, b, s3[:, kc, :], src.ap, [sl, src], kc == 0, kc == 15)
                hv = V(self.Hc(n).ap[:, th * TH:(th + 1) * TH], Reg("sb", A_H + n * 1024 + th * TH, A_H + n * 1024 + (th + 1) * TH))
                self.op("dve", lambda e, hv=hv, b=b: e.tensor_tensor(out=hv.ap, in0=hv.ap, in1=b.ap, op=ALU.add), reads=[hv, b], writes=[hv])
            self.ring_rel(i)
        return slab["i"]

    def finish(self):
        nc = self.nc
        with contextlib.ExitStack() as st:
            sems = {e: st.enter_context(nc.semaphore("s_" + e)) for e in ENGS}
            dsem = {k: st.enter_context(nc.semaphore("d_" + k)) for k in self.S.dma_tot.keys()}
            block = st.enter_context(nc.Block())
            be = {"pe": block.tensor, "act": block.scalar, "dve": block.vector, "pool": block.gpsimd, "sp": block.sync}
            self.stats = self.S.emit(be, sems, dsem)
        self.stack.close()
        return nc


def _consts():
    cst = np.zeros((128, NCST), np.float32)
    cst[:, 0:128] = np.eye(128, dtype=np.float32)
    cst[:, 128:256] = 1.0
    pp = np.arange(128)[:, None] % 64
    tt = np.arange(256)[None, :] % 64
    cst[:, 256:512] = (pp <= tt).astype(np.float32)
    return cst


def _prm(I, core):
    prm = np.zeros((128, NPRM_IN), np.float32)

    def col16(v):
        return np.asarray(v, np.float32).reshape(-1, 128).T

    prm[:, PC_G1:PC_G1 + 16] = col16(I["norm_ffn1"][0])
    prm[:, PC_GM:PC_GM + 16] = col16(I["norm_mix"][0])
    prm[:, PC_G2:PC_G2 + 16] = col16(I["norm_ffn2"][0])
    prm[:, PC_GP:PC_GP + 16] = col16(I["norm_ple"][0])
    prm[:, PC_GF:PC_GF + 16] = col16(I["norm_final"])
    prm[:, PC_L0:PC_L0 + 8] = col16(I["hgrn_lb_logits"][0])
    prm[:, PC_L1:PC_L1 + 8] = col16(I["hgrn_lb_logits"][1])
    prm[:, PC_HGN] = np.asarray(I["hgrn_norm"][0], np.float32)
    prm[:, PC_CVN:PC_CVN + 8] = col16(I["conv_norm"][0])
    for j in range(3):
        prm[:, PC_CW + 8 * j:PC_CW + 8 * j + 8] = col16(I["conv_w"][0][j])
    prm[:, PC_FLAG] = float(core % 2)
    prm[:, PC_EPS] = EPS
    return prm


_CACHE = {}


STOP = None
SKIP1 = False


def _get_program(debug=False):
    key = ("prog", debug, STOP, SKIP1)
    if key not in _CACHE:
        ns = None
        if STOP is not None:
            b0 = Builder(debug=False, stop=STOP, skip1=SKIP1)
            b0.build()
            ns = b0.used_slabs
            b0.stack.close()
        b = Builder(debug=debug, stop=STOP, nslab=ns, skip1=SKIP1)
        b.build()
        b.finish()
        _CACHE[key] = b
    return _CACHE[key]


def kernel(x, p, norm_ffn1, ffn1_gate, ffn1_up, ffn1_down, norm_mix, w_in, conv_w, hgrn_lb_logits, hgrn_norm,
           conv_norm, w_out, norm_ffn2, ffn2_gate, ffn2_up, ffn2_down, norm_ple, w_ple, w_ple_gate, norm_final,
           _debug=False):
    I = dict(norm_ffn1=np.asarray(norm_ffn1), norm_mix=np.asarray(norm_mix), norm_ffn2=np.asarray(norm_ffn2),
             norm_ple=np.asarray(norm_ple), norm_final=np.asarray(norm_final), hgrn_lb_logits=np.asarray(hgrn_lb_logits),
             hgrn_norm=np.asarray(hgrn_norm), conv_norm=np.asarray(conv_norm), conv_w=np.asarray(conv_w))
    W = {"ffn1_gate": np.asarray(ffn1_gate)[0], "ffn1_up": np.asarray(ffn1_up)[0], "ffn1_down": np.asarray(ffn1_down)[0],
         "ffn2_gate": np.asarray(ffn2_gate)[0], "ffn2_up": np.asarray(ffn2_up)[0], "ffn2_down": np.asarray(ffn2_down)[0],
         "w_in": np.asarray(w_in)[0], "w_out": np.asarray(w_out)[0], "w_ple": np.asarray(w_ple)[0],
         "w_ple_gate": np.asarray(w_ple_gate)[0]}
    b = _get_program(_debug)
    wts = build_slabs(b.plan[b.slab_off:b.nslab], W)
    cst = _consts()
    x = np.asarray(x, np.float32)
    p = np.asarray(p, np.float32)
    in_maps = []
    for c in range(NCORES):
        bi, half = divmod(c, 2)
        xs = x[bi, half * T:(half + 1) * T, :]
        xT = np.ascontiguousarray(xs.reshape(T, 16, 128).transpose(2, 1, 0)).reshape(128, 16 * T)
        ps_ = p[0, bi, half * T:(half + 1) * T, :]
        pT = np.ascontiguousarray(ps_.reshape(T, 2, 128).transpose(2, 1, 0)).reshape(128, 2 * T)
        in_maps.append({"xT": xT, "pT": pT, "prm": _prm(I, c), "cst": cst, "wts": wts})
    res = run_bass_kernel_spmd(b.nc, in_maps, core_ids=list(range(NCORES)))
    out = np.empty((4, 2 * T, D), np.float32)
    for c in range(NCORES):
        bi, half = divmod(c, 2)
        yT = res.results[c]["yT"].reshape(128, 16, T)
        out[bi, half * T:(half + 1) * T, :] = yT.transpose(2, 1, 0).reshape(T, D)
    if _debug:
        return out, res.results
    return out
```

```python
import contextlib
import numpy as np
import concourse.bass as bass
import concourse.mybir as mybir
from concourse.bass_utils import run_bass_kernel_spmd

F32 = mybir.dt.float32
BF16 = mybir.dt.bfloat16
AF = mybir.ActivationFunctionType
ALU = mybir.AluOpType

ENGS = ("pe", "act", "dve", "pool", "sp")

D = 2048
T = 1024
TH = 512
NCH = 16
DFF = 5632
NFC = 44
GRP = 4
NGRP = 11
EPS = 1e-6
NCORES = 8


class Op:
    __slots__ = ("eng", "fn", "deps", "needs_inc", "inc_val", "dma_key", "dma_val", "dma_inc", "name")

    def __init__(self, eng, fn, name=""):
        self.eng = eng
        self.fn = fn
        self.deps = []
        self.needs_inc = False
        self.inc_val = None
        self.dma_key = None
        self.dma_val = None
        self.dma_inc = 16
        self.name = name


class Reg:
    __slots__ = ("space", "lo", "hi")

    def __init__(self, space, lo, hi):
        self.space = space
        self.lo = lo
        self.hi = hi


class Sched:
    def __init__(self):
        self.ops = {e: [] for e in ENGS}
        self.ent = {}
        self.dma_tot = {}
        self.dma_group = set()

    def _overl(self, reg):
        lst = self.ent.setdefault(reg.space, [])
        return [e for e in lst if e[0] < reg.hi and reg.lo < e[1]]

    def add(self, eng, fn, reads=(), writes=(), dma_key=None, dma_inc=16, name=""):
        op = Op(eng, fn, name)
        deps = {}
        for r in reads:
            for e in self._overl(r):
                if e[2] is not None:
                    deps[id(e[2])] = (e[2], "raw")
        for w in writes:
            for e in self._overl(w):
                if e[2] is not None and id(e[2]) not in deps:
                    deps[id(e[2])] = (e[2], "waw")
                for rd in e[3]:
                    if id(rd) not in deps:
                        deps[id(rd)] = (rd, "war")
        for d, kind in deps.values():
            if d is op:
                continue
            if d.dma_key is None and dma_key is None and d.eng == eng:
                if eng == "pe":
                    continue
            op.deps.append(d)
            d.needs_inc = True
        for w in writes:
            lst = self.ent.setdefault(w.space, [])
            keep = [e for e in lst if not (w.lo <= e[0] and e[1] <= w.hi)]
            keep.append([w.lo, w.hi, op, []])
            self.ent[w.space] = keep
        for r in reads:
            found = False
            for e in self._overl(r):
                e[3].append(op)
                if e[0] <= r.lo and r.hi <= e[1]:
                    found = True
            if not found:
                self.ent.setdefault(r.space, []).append([r.lo, r.hi, None, [op]])
        if dma_key is not None:
            op.dma_key = dma_key
            op.dma_inc = dma_inc
            self.dma_tot[dma_key] = self.dma_tot.get(dma_key, 0) + dma_inc
            op.dma_val = self.dma_tot[dma_key]
        self.ops[eng].append(op)
        return op

    def emit(self, block_engines, sems, dma_sems):
        for e in ENGS:
            n = 0
            for op in self.ops[e]:
                if op.dma_key is None and op.needs_inc:
                    n += 1
                    op.inc_val = n
        for key in self.dma_group:
            for e in ENGS:
                for op in self.ops[e]:
                    if op.dma_key == key:
                        op.dma_val = self.dma_tot[key]
        stats = {}

        def body_for(e):
            def body(eng):
                seen = {}
                nw = 0
                for op in self.ops[e]:
                    need = {}
                    for d in op.deps:
                        if d.dma_key is not None:
                            k, v = ("dma", d.dma_key), d.dma_val
                        else:
                            k, v = ("eng", d.eng), d.inc_val
                        if v > need.get(k, 0):
                            need[k] = v
                    for k, v in need.items():
                        if seen.get(k, 0) >= v:
                            continue
                        seen[k] = v
                        eng.wait_ge(dma_sems[k[1]] if k[0] == "dma" else sems[k[1]], v)
                        nw += 1
                    ins = op.fn(eng)
                    if ins is None:
                        continue
                    if op.dma_key is not None:
                        ins.then_inc(dma_sems[op.dma_key], op.dma_inc)
                    elif op.needs_inc:
                        ins.then_inc(sems[e], 1)
                stats[e] = (len(self.ops[e]), nw)
            return body

        for e in ENGS:
            if self.ops[e]:
                block_engines[e](body_for(e))
        return stats


class V:
    __slots__ = ("ap", "reg")

    def __init__(self, ap, reg):
        self.ap = ap
        self.reg = reg


def slab_plan():
    plan = []

    def ffn(tag):
        def gu(g):
            for fl in range(GRP):
                f = g * GRP + fl
                plan.append((tag + "_gate", "k", f))
                plan.append((tag + "_up", "k", f))

        def dn(g):
            for fl in range(GRP):
                plan.append((tag + "_down", "d", g * GRP + fl))
        gu(0)
        for g in range(1, NGRP):
            gu(g)
            dn(g - 1)
        dn(NGRP - 1)

    ffn("ffn1")
    mix_start = len(plan)
    for hd in range(8):
        plan.append(("w_in", "k", 0 + hd))
        plan.append(("w_in", "k", 8 + hd))
        plan.append(("w_in", "k", 16 + hd))
    for c in range(8):
        plan.append(("w_in", "k", 32 + c))
        plan.append(("w_in", "k", 40 + c))
        plan.append(("w_in", "k", 48 + c))
    for hd in range(8):
        plan.append(("w_in", "k", 24 + hd))
    for n in range(16):
        plan.append(("w_out", "k", n))
    mix_end = len(plan)
    ffn("ffn2")
    plan.append(("w_ple", "d", 0))
    plan.append(("w_ple", "d", 1))
    for n in range(16):
        plan.append(("w_ple_gate", "k", n))
    return plan, mix_start, mix_end


def build_slabs(plan, W):
    out = np.empty((len(plan), 128, 2048), np.float32)
    for i, (name, kind, idx) in enumerate(plan):
        w = W[name]
        if kind == "k":
            blk = w[:, idx * 128:(idx + 1) * 128]
            out[i] = blk.reshape(16, 128, 128).transpose(1, 0, 2).reshape(128, 2048)
        else:
            out[i] = w[idx * 128:(idx + 1) * 128, :]
    return out


PC_G1, PC_GM, PC_G2, PC_GP, PC_GF = 0, 16, 32, 48, 64
PC_L0, PC_L1 = 80, 88
PC_HGN = 96
PC_CVN = 97
PC_CW = 105
PC_FLAG = 129
PC_EPS = 130
NPRM_IN = 131
PC_LB, PC_OML, PC_NOML = 131, 139, 147
NPRM = 160
NCST = 512

A_PRM = 0
A_CSTB = 160
A_H = 416
A_X = A_H + 16384
A_RING = A_X + 8192
NSLOT = 12
NSLOT_MIX = 6
A_MIX = A_RING + NSLOT_MIX * 1024
A_PHASE = A_RING + NSLOT * 1024
A_END = 52000


class Builder:
    def __init__(self, debug=False, stop=None, nslab=None, skip1=False):
        self.debug = debug
        self.stop = stop
        self.skip1 = skip1
        self.used_slabs = None
        self.taps = []
        nc = bass.Bass("TRN2", target_bir_lowering=False)
        self.nc = nc
        self.S = Sched()
        self.plan, self.mix_start, self.mix_end = slab_plan()
        ns = len(self.plan) if nslab is None else nslab
        self.nslab = ns
        self.slab_off = self.mix_start if skip1 else 0
        self.d_x = nc.dram_tensor("xT", [128, 16 * T], F32, kind="ExternalInput")
        self.d_p = nc.dram_tensor("pT", [128, 2 * T], F32, kind="ExternalInput")
        self.d_prm = nc.dram_tensor("prm", [128, NPRM_IN], F32, kind="ExternalInput")
        self.d_cst = nc.dram_tensor("cst", [128, NCST], F32, kind="ExternalInput")
        self.d_w = nc.dram_tensor("wts", [ns - self.slab_off, 128, 2048], F32, kind="ExternalInput")
        self.d_y = nc.dram_tensor("yT", [128, 16 * T], F32, kind="ExternalOutput")
        self.d_s1 = nc.dram_tensor("send1", [128, 1024], F32)
        self.d_r1 = nc.dram_tensor("recv1", [256, 1024], F32)
        self.d_s2 = nc.dram_tensor("send2", [128, 16], F32)
        self.d_r2 = nc.dram_tensor("recv2", [256, 16], F32)
        self.stack = contextlib.ExitStack()
        self.sb = self.stack.enter_context(nc.sbuf_tensor("arena", [128, A_END], F32))
        self.ps = self.stack.enter_context(nc.psum_tensor("psum", [128, 4096], F32))
        self.cnt = 0

    def f32(self, lo, n):
        return V(self.sb[:, lo:lo + n], Reg("sb", lo, lo + n))

    def bf(self, lo, nwords):
        return V(self.sb[:, lo:lo + nwords].bitcast(BF16), Reg("sb", lo, lo + nwords))

    def bank(self, b):
        return V(self.ps[:, b * 512:(b + 1) * 512], Reg("ps", b * 512, (b + 1) * 512))

    def prm(self, col, n=1):
        return V(self.sb[:, A_PRM + col:A_PRM + col + n], Reg("sb", A_PRM + col, A_PRM + col + n))

    def op(self, eng, fn, reads=(), writes=(), **kw):
        return self.S.add(eng, fn, reads=[r.reg if isinstance(r, V) else r for r in reads],
                          writes=[w.reg if isinstance(w, V) else w for w in writes], **kw)

    def tap(self, name, v, ncols, dt=F32):
        if not self.debug:
            return
        d = self.nc.dram_tensor("dbg_" + name, [128, ncols], dt, kind="ExternalOutput")
        self.taps.append("dbg_" + name)
        step = 1024
        for c0 in range(0, ncols, step):
            c1 = min(ncols, c0 + step)
            self.op("sp", lambda e, c0=c0, c1=c1: e.dma_start(out=d[:, c0:c1], in_=v.ap[:, c0:c1]), reads=[v], dma_key="tap_" + name)
        self.S.dma_group.add("tap_" + name)

    def ring_init(self):
        self.r_free = list(range(NSLOT))
        self.r_next = self.slab_off
        self.r_hold = True
        self.r_slot = {}
        self.ring_fill()

    def slot_view(self, s):
        return self.bf(A_RING + s * 1024, 1024)

    def ring_fill(self):
        while self.r_next < self.nslab:
            i = self.r_next
            mix = self.mix_start <= i < self.mix_end
            cand = [s for s in self.r_free if (not mix) or s < NSLOT_MIX]
            if not cand:
                return
            s = cand[0]
            self.r_free.remove(s)
            self.r_slot[i] = s
            v = self.slot_view(s)
            src = self.d_w[i - self.slab_off]
            rd = [self.f32(A_H, 16384)] if (self.r_hold and i < self.slab_off + 3) else []
            self.op("pool", lambda e, v=v, src=src: e.dma_start(out=v.ap, in_=src), reads=rd, writes=[v], dma_key="r%d" % s)
            self.r_next += 1

    def ring_get(self, i):
        assert i in self.r_slot, "ring too small / order violated at slab %d (%s)" % (i, self.plan[i])
        return self.slot_view(self.r_slot[i])

    def ring_rel(self, i):
        self.r_free.append(self.r_slot.pop(i))
        self.r_free.sort()
        self.ring_fill()

    def mm(self, out_ap, out_v, lhsT_ap, rhs_ap, reads, start, stop):
        self.op("pe", lambda e: e.matmul(out_ap, lhsT=lhsT_ap, rhs=rhs_ap, start=start, stop=stop),
                reads=reads, writes=[out_v])

    def rmsnorm_sums(self, src_chunks, nparts_scale, sq_bufs, bsum):
        n = len(src_chunks)
        for c, sv in enumerate(src_chunks):
            sq = sq_bufs[c % len(sq_bufs)]
            if c % 2 == 0:
                self.op("act", lambda e, sq=sq, sv=sv: e.activation(out=sq.ap, in_=sv.ap, func=AF.Square), reads=[sv], writes=[sq])
            else:
                self.op("dve", lambda e, sq=sq, sv=sv: e.tensor_tensor(out=sq.ap, in0=sv.ap, in1=sv.ap, op=ALU.mult), reads=[sv], writes=[sq])
            for th in range(2):
                b = bsum[th]
                self.mm(b.ap, b, self.ONESB.ap, sq.ap[:, th * TH:(th + 1) * TH], [self.ONESB, sq], c == 0, c == n - 1)

    def norm_to_X(self, gcol):
        R = self.f32(self.a_R, 1024)
        sqs = [self.bf(self.a_SQ + i * 512, 512) for i in range(2)] + [self.bf(self.a_R + 1024 + i * 512, 512) for i in range(2)]
        bs = [self.bank(6), self.bank(7)]
        self.rmsnorm_sums([self.Hc(c) for c in range(16)], 1.0 / D, sqs, bs)
        eps = self.prm(PC_EPS)
        for th in range(2):
            self.op("act", lambda e, th=th: e.activation(out=R.ap[:, th * TH:(th + 1) * TH], in_=bs[th].ap, func=AF.Ln,
                                                         scale=1.0 / D, bias=eps.ap), reads=[bs[th], eps], writes=[R])
        self.op("act", lambda e: e.activation(out=R.ap, in_=R.ap, func=AF.Exp, scale=-0.5), reads=[R], writes=[R])
        for th in range(2):
            for c in range(16):
                g = self.prm(gcol + c)
                xh = self.Xh(c, th)
                hc = self.Hc(c)
                self.op("dve", lambda e, g=g, xh=xh, hc=hc, th=th: e.scalar_tensor_tensor(
                    out=xh.ap, in0=hc.ap[:, th * TH:(th + 1) * TH], scalar=g.ap, in1=R.ap[:, th * TH:(th + 1) * TH],
                    op0=ALU.mult, op1=ALU.mult), reads=[hc, g, R], writes=[xh])

    def Hc(self, c):
        return self.f32(A_H + c * 1024, 1024)

    def Xc(self, c):
        return self.bf(A_X + c * 512, 512)

    def Xh(self, c, th):
        return self.bf(A_X + c * 512 + th * 256, 256)

    def ffn(self, base, gcol):
        self.a_R = A_PHASE + 6144
        self.a_SQ = A_PHASE + 5120
        self.norm_to_X(gcol)
        HT = [self.bf(A_PHASE + i * 2048, 2048) for i in range(2)]
        T1 = [self.f32(A_PHASE + 4096 + i * 512, 512) for i in range(2)]
        st = {"gu": 0, "dn": 0, "slab": base}

        def take():
            i = st["slab"]
            st["slab"] += 1
            return i

        def gu(g):
            ht = HT[g % 2]
            for fl in range(GRP):
                ig, iu = take(), take()
                sg, su = self.ring_get(ig), self.ring_get(iu)
                sg3 = sg.ap.rearrange("p (k n) -> p k n", n=128)
                su3 = su.ap.rearrange("p (k n) -> p k n", n=128)
                for th in range(2):
                    u = st["gu"]
                    st["gu"] += 1
                    bg, bu = self.bank(2 * (u % 3)), self.bank(2 * (u % 3) + 1)
                    for kc in range(16):
                        xk = self.Xh(kc, th)
                        self.mm(bg.ap, bg, sg3[:, kc, :], xk.ap, [sg, xk], kc == 0, kc == 15)
                    for kc in range(16):
                        xk = self.Xh(kc, th)
                        self.mm(bu.ap, bu, su3[:, kc, :], xk.ap, [su, xk], kc == 0, kc == 15)
                    t1 = T1[u % 2]
                    self.op("act", lambda e, t1=t1, bg=bg: e.activation(out=t1.ap, in_=bg.ap, func=AF.Silu), reads=[bg], writes=[t1])
                    lo = fl * 1024 + th * TH
                    hv = V(ht.ap[:, lo:lo + TH], Reg("sb", ht.reg.lo + lo // 2, ht.reg.lo + (lo + TH) // 2))
                    self.op("dve", lambda e, hv=hv, t1=t1, bu=bu: e.tensor_tensor(out=hv.ap, in0=t1.ap, in1=bu.ap, op=ALU.mult),
                            reads=[t1, bu], writes=[hv])
                self.ring_rel(ig)
                self.ring_rel(iu)

        def dn(g):
            ht = HT[g % 2]
            ids = [take() for _ in range(GRP)]
            sl = [self.ring_get(i) for i in ids]
            for n in range(16):
                for th in range(2):
                    b = self.bank((6 + st["dn"]) % 8)
                    st["dn"] += 1
                    for fl in range(GRP):
                        lo = fl * 1024 + th * TH
                        self.mm(b.ap, b, sl[fl].ap[:, n * 128:(n + 1) * 128], ht.ap[:, lo:lo + TH], [sl[fl], ht], fl == 0, fl == GRP - 1)
                    hv = V(self.Hc(n).ap[:, th * TH:(th + 1) * TH], Reg("sb", A_H + n * 1024 + th * TH, A_H + n * 1024 + (th + 1) * TH))
                    self.op("dve", lambda e, hv=hv, b=b: e.scalar_tensor_tensor(
                        out=hv.ap, in0=b.ap, scalar=0.5, in1=hv.ap, op0=ALU.mult, op1=ALU.add), reads=[b, hv], writes=[hv])
            for i in ids:
                self.ring_rel(i)

        gu(0)
        for g in range(1, NGRP):
            gu(g)
            dn(g - 1)
        dn(NGRP - 1)
        return st["slab"]

    def build(self):
        S = self.S
        PRM = self.f32(A_PRM, NPRM_IN)
        self.op("sp", lambda e: e.dma_start(out=PRM.ap, in_=self.d_prm[:, :]), writes=[PRM], dma_key="ld")
        CSTB = self.bf(A_CSTB, 256)
        self.op("pool", lambda e: e.dma_start(out=CSTB.ap, in_=self.d_cst[:, :]), writes=[CSTB], dma_key="ldc")
        self.IDB = V(CSTB.ap[:, 0:128], CSTB.reg)
        self.ONESB = V(CSTB.ap[:, 128:256], CSTB.reg)
        self.MASK = V(CSTB.ap[:, 256:512], CSTB.reg)
        for c in range(16):
            hc = self.Hc(c)
            self.op("sp", lambda e, hc=hc, c=c: e.dma_start(out=hc.ap, in_=self.d_x[:, c * T:(c + 1) * T]), writes=[hc], dma_key="lx%d" % c)
        S.dma_group.add("ld")
        self.ring_init()
        LB, OML, NOML = self.prm(PC_LB, 8), self.prm(PC_OML, 8), self.prm(PC_NOML, 8)
        L0, L1 = self.prm(PC_L0, 8), self.prm(PC_L1, 8)
        self.op("dve", lambda e: e.tensor_tensor(out=LB.ap, in0=L0.ap, in1=L1.ap, op=ALU.subtract), reads=[L0, L1], writes=[LB])
        self.op("act", lambda e: e.activation(out=LB.ap, in_=LB.ap, func=AF.Sigmoid), reads=[LB], writes=[LB])
        self.op("dve", lambda e: e.tensor_scalar(out=OML.ap, in0=LB.ap, scalar1=-1.0, scalar2=1.0, op0=ALU.mult, op1=ALU.add), reads=[LB], writes=[OML])
        self.op("dve", lambda e: e.tensor_scalar(out=NOML.ap, in0=OML.ap, scalar1=-1.0, scalar2=None, op0=ALU.mult), reads=[OML], writes=[NOML])

        if self.skip1:
            nxt = self.mix_start
        else:
            nxt = self.ffn(0, PC_G1)
        self.tap("h1", self.f32(A_H, 16384), 16384)
        if self.stop == "ffn1":
            self.used_slabs = nxt
            return self.final()
        nxt = self.mixer(nxt)
        self.tap("h2", self.f32(A_H, 16384), 16384)
        if self.stop is not None and self.stop != "ffn2":
            self.used_slabs = max(nxt, self.r_next)
            return self.final()
        PT = self.bf(A_PHASE + 8192, 1024)
        self.op("pool", lambda e: e.dma_start(out=PT.ap, in_=self.d_p[:, :]), writes=[PT], dma_key="ldp")
        nxt = self.ffn(nxt, PC_G2)
        self.tap("h3", self.f32(A_H, 16384), 16384)
        if self.stop == "ffn2":
            self.used_slabs = nxt
            return self.final()
        nxt = self.ple(nxt, PT)
        assert nxt == len(self.plan)
        self.final()

    def ple(self, base, PT):
        self.norm_to_X(PC_GP)
        i0, i1 = base, base + 1
        sp = [self.ring_get(i0), self.ring_get(i1)]
        T1 = [self.f32(A_PHASE + 4096 + i * 512, 512) for i in range(2)]
        u = 0
        for n in range(16):
            i = base + 2 + n
            s = self.ring_get(i)
            s3 = s.ap.rearrange("p (k n) -> p k n", n=128)
            for th in range(2):
                ba, bb = self.bank(2 * (u % 3)), self.bank(2 * (u % 3) + 1)
                for kc in range(16):
                    xk = self.Xc(kc)
                    self.mm(ba.ap, ba, s3[:, kc, :], xk.ap[:, th * TH:(th + 1) * TH], [s, xk], kc == 0, kc == 15)
                for kc in range(2):
                    self.mm(bb.ap, bb, sp[kc].ap[:, n * 128:(n + 1) * 128], PT.ap[:, kc * T + th * TH:kc * T + (th + 1) * TH],
                            [sp[kc], PT], kc == 0, kc == 1)
                t1 = T1[u % 2]
                self.op("act", lambda e, t1=t1, ba=ba: e.activation(out=t1.ap, in_=ba.ap, func=AF.Sigmoid), reads=[ba], writes=[t1])
                self.op("dve", lambda e, t1=t1, bb=bb: e.tensor_tensor(out=t1.ap, in0=t1.ap, in1=bb.ap, op=ALU.mult), reads=[t1, bb], writes=[t1])
                hv = V(self.Hc(n).ap[:, th * TH:(th + 1) * TH], Reg("sb", A_H + n * 1024 + th * TH, A_H + n * 1024 + (th + 1) * TH))
                self.op("dve", lambda e, hv=hv, t1=t1: e.tensor_tensor(out=hv.ap, in0=hv.ap, in1=t1.ap, op=ALU.add), reads=[hv, t1], writes=[hv])
                u += 1
            self.ring_rel(i)
        self.ring_rel(i0)
        self.ring_rel(i1)
        return base + 18

    def final(self):
        R = self.f32(self.a_R, 1024)
        sqs = [self.bf(self.a_SQ + i * 512, 512) for i in range(2)]
        bs = [self.bank(6), self.bank(7)]
        self.rmsnorm_sums([self.Hc(c) for c in range(16)], 1.0 / D, sqs, bs)
        eps = self.prm(PC_EPS)
        for th in range(2):
            self.op("act", lambda e, th=th: e.activation(out=R.ap[:, th * TH:(th + 1) * TH], in_=bs[th].ap, func=AF.Ln,
                                                         scale=1.0 / D, bias=eps.ap), reads=[bs[th], eps], writes=[R])
        self.op("act", lambda e: e.activation(out=R.ap, in_=R.ap, func=AF.Exp, scale=-0.5), reads=[R], writes=[R])
        OB = [self.f32(A_PHASE + i * 1024, 1024) for i in range(4)]
        outs = []
        for c in range(16):
            g = self.prm(PC_GF + c)
            ob = OB[c % 4]
            hc = self.Hc(c)
            self.op("dve", lambda e, g=g, ob=ob, hc=hc: e.scalar_tensor_tensor(
                out=ob.ap, in0=hc.ap, scalar=g.ap, in1=R.ap, op0=ALU.mult, op1=ALU.mult), reads=[hc, g, R], writes=[ob])
            outs.append(self.op("sp", lambda e, ob=ob, c=c: e.dma_start(out=self.d_y[:, c * T:(c + 1) * T], in_=ob.ap),
                                reads=[ob], dma_key="out%d" % (c % 4)))
        fin = self.S.add("sp", lambda e: None, name="final")
        for e in ENGS:
            for op in self.S.ops[e]:
                if op.dma_key is not None and (op.dma_key.startswith("out") or op.dma_key.startswith("tap")):
                    fin.deps.append(op)

    def mixer(self, base):
        self.a_R = A_PHASE + 6144
        self.a_SQ = A_PHASE + 5120
        self.norm_to_X(PC_GM)
        a = A_MIX
        a_OLOC = a
        a += 8192
        a_QH = a
        a += 4096
        a_Y = a
        a += 4096
        a_SCR = a
        a_HSCR = a_Y
        assert A_END - a_SCR >= 4200

        def OLOC(hd, th):
            lo = a_OLOC + hd * 1024 + th * TH
            return self.f32(lo, TH)

        def QH(hd, th=None):
            if th is None:
                return self.bf(a_QH + hd * 512, 512)
            return self.bf(a_QH + hd * 512 + th * 256, 256)

        def Yv(c, th=None):
            if th is None:
                return self.bf(a_Y + c * 512, 512)
            return self.bf(a_Y + c * 512 + th * 256, 256)

        slab = {"i": base}

        def take():
            i = slab["i"]
            slab["i"] += 1
            return i

        p = a_HSCR
        sets = []
        for s in range(2):
            d = {}
            d["A2"] = self.f32(p, 512); p += 512
            d["A3"] = self.f32(p, 512); p += 512
            d["A4"] = self.f32(p, 512); p += 512
            d["QT"] = self.bf(p, 256); p += 256
            d["KT"] = self.bf(p, 256); p += 256
            d["KTM"] = self.bf(p, 256); p += 256
            d["VTM"] = self.bf(p, 256); p += 256
            d["S1B"] = self.bf(p, 512); p += 512
            d["PSB"] = self.bf(p, 256); p += 256
            d["DD"] = self.f32(p, 24); p += 24
            d["EE"] = self.f32(p, 24); p += 24
            sets.append(d)
        SR = []
        for i in range(4):
            SR.append(self.f32(p, 128)); p += 128
        SEND = self.f32(p, 1024); p += 1024
        assert p <= A_END, p
        hslabs = {}

        def hg_P(u, part=2):
            hd, th = divmod(u, 2)
            s = u % 2
            if th == 0 and part != 1:
                hslabs[hd] = (take(), take(), take())
            iq, if_, ii = hslabs[hd]
            sq, sf, si = self.ring_get(iq), self.ring_get(if_), self.ring_get(ii)
            sq3 = sq.ap.rearrange("p (k n) -> p k n", n=128)
            sf3 = sf.ap.rearrange("p (k n) -> p k n", n=128)
            si3 = si.ap.rearrange("p (k n) -> p k n", n=128)
            bq, bf_, bv = self.bank(4 * s), self.bank(4 * s + 1), self.bank(4 * s + 2)
            if part != 1:
                for kc in range(16):
                    xk = self.Xc(kc)
                    self.mm(bq.ap, bq, sq3[:, kc, :], xk.ap[:, th * TH:(th + 1) * TH], [sq, xk], kc == 0, kc == 15)
                for kc in range(16):
                    xk = self.Xc(kc)
                    self.mm(bf_.ap, bf_, sf3[:, kc, :], xk.ap[:, th * TH:(th + 1) * TH], [sf, xk], kc == 0, kc == 15)
            if part == 0:
                return
            for tt in range(4):
                t0 = th * TH + tt * 128
                for kc in range(16):
                    xk = self.Xc(kc)
                    self.mm(bv.ap[:, tt * 128:(tt + 1) * 128], bv, xk.ap[:, t0:t0 + 128], si3[:, kc, :], [si, xk], kc == 0, kc == 15)
            if th == 1:
                for i in hslabs[hd]:
                    self.ring_rel(i)

        def hg_E(u):
            hd, th = divmod(u, 2)
            s = u % 2
            d = sets[s]
            dp = sets[1 - s]
            bq, bf_, bv = self.bank(4 * s), self.bank(4 * s + 1), self.bank(4 * s + 2)
            A2, A3, A4, QT, KT, VTM, DD, EE = d["A2"], d["A3"], d["A4"], d["QT"], d["KT"], d["VTM"], d["DD"], d["EE"]
            lb, oml, noml = self.prm(PC_LB + hd), self.prm(PC_OML + hd), self.prm(PC_NOML + hd)
            op = self.op
            op("act", lambda e: e.activation(out=bf_.ap, in_=bf_.ap, func=AF.Sigmoid), reads=[bf_], writes=[bf_])
            op("act", lambda e: e.activation(out=A4.ap, in_=bq.ap, func=AF.Sigmoid), reads=[bq], writes=[A4])
            op("act", lambda e: e.activation(out=A2.ap, in_=bf_.ap, func=AF.Ln, scale=oml.ap, bias=lb.ap), reads=[bf_, oml, lb], writes=[A2])
            op("dve", lambda e: e.tensor_tensor(out=bq.ap, in0=bq.ap, in1=A4.ap, op=ALU.mult), reads=[bq, A4], writes=[bq])
            op("dve", lambda e: e.tensor_scalar(out=bf_.ap, in0=bf_.ap, scalar1=noml.ap, scalar2=oml.ap, op0=ALU.mult, op1=ALU.add),
               reads=[bf_, noml, oml], writes=[bf_])
            if th == 0:
                op("dve", lambda e: e.tensor_tensor_scan(out=A3.ap, data0=A2.ap, data1=A2.ap, initial=0.0, op0=ALU.add, op1=ALU.bypass),
                   reads=[A2], writes=[A3])
            else:
                pa3 = dp["A3"]
                op("dve", lambda e: e.tensor_tensor_scan(out=A3.ap, data0=A2.ap, data1=A2.ap, initial=pa3.ap[:, 511:512], op0=ALU.add, op1=ALU.bypass),
                   reads=[A2, pa3], writes=[A3])
            A3v = A3.ap.rearrange("p (c t) -> p c t", t=64)
            A2v = A2.ap.rearrange("p (c t) -> p c t", t=64)
            op("dve", lambda e: e.tensor_tensor(out=A2v, in0=A3v, in1=A3v[:, :, 31:32].broadcast_to([128, 8, 64]), op=ALU.subtract),
               reads=[A3], writes=[A2])
            op("act", lambda e: e.activation(out=A4.ap, in_=A2.ap, func=AF.Exp), reads=[A2], writes=[A4])
            op("dve", lambda e: e.tensor_tensor(out=QT.ap, in0=bq.ap, in1=A4.ap, op=ALU.mult), reads=[bq, A4], writes=[QT])
            op("act", lambda e: e.activation(out=A4.ap, in_=A2.ap, func=AF.Exp, scale=-1.0), reads=[A2], writes=[A4])
            op("dve", lambda e: e.tensor_tensor(out=KT.ap, in0=bf_.ap, in1=A4.ap, op=ALU.mult), reads=[bf_, A4], writes=[KT])
            op("act", lambda e: e.activation(out=A4.ap, in_=A3.ap, func=AF.Exp), reads=[A3], writes=[A4])
            qh = QH(hd, th)
            op("dve", lambda e: e.tensor_tensor(out=qh.ap, in0=bq.ap, in1=A4.ap, op=ALU.mult), reads=[bq, A4], writes=[qh])
            DDv = DD.ap.rearrange("p (j c) -> p j c", c=8)
            Bref = A3v[:, :, 31]
            Blast = A3v[:, :, 63]
            if th == 0:
                op("dve", lambda e: e.tensor_copy(out=DDv[:, 0, 0:1], in_=Bref[:, 0:1]), reads=[A3], writes=[DD])
                op("dve", lambda e: e.tensor_copy(out=DDv[:, 2, 0:1], in_=Blast[:, 0:1]), reads=[A3], writes=[DD])
            else:
                pa3 = dp["A3"]
                op("dve", lambda e: e.tensor_tensor(out=DDv[:, 0, 0:1], in0=Bref[:, 0:1], in1=pa3.ap[:, 511:512], op=ALU.subtract), reads=[A3, pa3], writes=[DD])
                op("dve", lambda e: e.tensor_tensor(out=DDv[:, 2, 0:1], in0=Blast[:, 0:1], in1=pa3.ap[:, 511:512], op=ALU.subtract), reads=[A3, pa3], writes=[DD])
            op("dve", lambda e: e.tensor_tensor(out=DDv[:, 0, 1:8], in0=Bref[:, 1:8], in1=Blast[:, 0:7], op=ALU.subtract), reads=[A3], writes=[DD])
            op("dve", lambda e: e.tensor_tensor(out=DDv[:, 1, :], in0=Blast, in1=Bref, op=ALU.subtract), reads=[A3], writes=[DD])
            op("dve", lambda e: e.tensor_tensor(out=DDv[:, 2, 1:8], in0=Blast[:, 1:8], in1=Blast[:, 0:7], op=ALU.subtract), reads=[A3], writes=[DD])
            op("act", lambda e: e.activation(out=EE.ap, in_=DD.ap, func=AF.Exp), reads=[DD], writes=[EE])
            op("act", lambda e: e.activation(out=VTM.ap, in_=bv.ap, func=AF.Copy), reads=[bv], writes=[VTM])

        def hg_small(u, part=0):
            hd, th = divmod(u, 2)
            s = u % 2
            d = sets[s]
            bq, bf_, bv, bm = self.bank(4 * s), self.bank(4 * s + 1), self.bank(4 * s + 2), self.bank(4 * s + 3)
            QT, KT, KTM, VTM, S1B, PSB, EE = d["QT"], d["KT"], d["KTM"], d["VTM"], d["S1B"], d["PSB"], d["EE"]
            op = self.op
            EEv = EE.ap.rearrange("p (j c) -> p j c", c=8)
            bmb = bm.ap.bitcast(BF16)
            S1v = S1B.ap.rearrange("p (c v) -> p c v", v=128)
            if part in (0, 1):
                for tt in range(4):
                    op("pe", lambda e, tt=tt: e.transpose(out=bmb[:, tt * 128:(tt + 1) * 128], in_=KT.ap[:, tt * 128:(tt + 1) * 128], identity=self.IDB.ap),
                       reads=[KT, self.IDB], writes=[bm])
                op("act", lambda e: e.activation(out=KTM.ap, in_=bmb[:, 0:512], func=AF.Copy), reads=[bm], writes=[KTM])
                if self.stop == "smA":
                    return
                for c in range(8):
                    tt, p0 = c // 2, (c % 2) * 64
                    bo = bq if c % 2 == 0 else bf_
                    op("pe", lambda e, bo=bo, c=c, tt=tt, p0=p0: e.matmul(bo.ap[:, tt * 128:(tt + 1) * 128],
                                                                        lhsT=KTM.ap[p0:p0 + 64, tt * 128:(tt + 1) * 128],
                                                                        rhs=VTM.ap[p0:p0 + 64, tt * 128:(tt + 1) * 128], start=True, stop=True),
                       reads=[KTM, VTM], writes=[bo])
                if self.stop == "smB1":
                    return
                for c in range(8):
                    p0 = (c % 2) * 64
                    col = 256 + (c // 2) * 64
                    op("pe", lambda e, c=c, p0=p0, col=col: e.matmul(bm.ap[p0:p0 + 64, col:col + 64], lhsT=KT.ap[:, c * 64:(c + 1) * 64],
                                                                    rhs=QT.ap[:, c * 64:(c + 1) * 64], start=True, stop=True),
                       reads=[KT, QT], writes=[bm])
                PSBv = PSB.ap.rearrange("p (j t) -> p j t", t=128)
                bmv = bm.ap[:, 256:512].rearrange("p (j t) -> p j t", t=64)
                if u < 2:
                    op("dve", lambda e: e.memset(PSB.ap, 0.0), writes=[PSB])
                for hf in range(2):
                    r0 = hf * 64
                    op("dve", lambda e, r0=r0: e.tensor_tensor(out=PSBv[r0:r0 + 64, :, r0:r0 + 64], in0=bmv[r0:r0 + 64, :, :],
                                                            in1=self.MASK.ap[r0:r0 + 64, 0:64].unsqueeze(1).broadcast_to([64, 4, 64]), op=ALU.mult),
                       reads=[bm, self.MASK], writes=[PSB])
                if self.stop == "smB":
                    return
                if th == 0:
                    op("dve", lambda e: e.memset(SR[3].ap, 0.0), writes=[SR[3]])
                S1v = S1B.ap.rearrange("p (c v) -> p c v", v=128)
                ELR = EEv[:, 1, :].rearrange("p (t two) -> p t two", two=2)
                for par, bo in ((0, bq), (1, bf_)):
                    bov = bo.ap.rearrange("p (t v) -> p t v", v=128)
                    op("dve", lambda e, par=par, bov=bov: e.tensor_tensor(out=bov, in0=bov, in1=ELR[:, :, par:par + 1].broadcast_to([128, 4, 128]), op=ALU.mult),
                       reads=[bo, EE], writes=[bo])
                for c in range(8):
                    bo = bq if c % 2 == 0 else bf_
                    pk = bo.ap[:, (c // 2) * 128:(c // 2 + 1) * 128]
                    prev, cur = SR[(c - 1) % 4], SR[c % 4]
                    op("act", lambda e, c=c, prev=prev: e.activation(out=S1v[:, c, :], in_=prev.ap, func=AF.Copy, scale=EEv[:, 0, c:c + 1]), reads=[prev, EE], writes=[S1B])
                    op("dve", lambda e, c=c, pk=pk, prev=prev, cur=cur: e.scalar_tensor_tensor(out=cur.ap, in0=prev.ap, scalar=EEv[:, 2, c:c + 1], in1=pk, op0=ALU.mult, op1=ALU.add),
                       reads=[prev, EE, bo], writes=[cur])
                if self.stop == "smC":
                    return
            if part in (0, 2):
                for tt in range(4):
                    op("pe", lambda e, tt=tt: e.matmul(bv.ap[:, tt * 128:(tt + 1) * 128], lhsT=VTM.ap[:, tt * 128:(tt + 1) * 128],
                                                      rhs=PSB.ap[:, tt * 128:(tt + 1) * 128], start=True, stop=False),
                       reads=[VTM, PSB], writes=[bv])
                    for c in (2 * tt, 2 * tt + 1):
                        op("pe", lambda e, c=c: e.matmul(bv.ap[:, c * 64:(c + 1) * 64], lhsT=S1v[:, c, :], rhs=QT.ap[:, c * 64:(c + 1) * 64],
                                                        start=False, stop=(c % 2 == 1)),
                           reads=[S1B, QT], writes=[bv])
                ol = OLOC(hd, th)
                op("act", lambda e: e.activation(out=ol.ap, in_=bv.ap, func=AF.Copy), reads=[bv], writes=[ol])
                if th == 1:
                    sv = V(SEND.ap[:, hd * 128:(hd + 1) * 128], Reg("sb", SEND.reg.lo + hd * 128, SEND.reg.lo + (hd + 1) * 128))
                    op("dve", lambda e: e.tensor_copy(out=sv.ap, in_=SR[3].ap), reads=[SR[3]], writes=[sv])

        hg_P(0)
        if self.stop == "hgP0":
            for i in hslabs[0]:
                self.ring_rel(i)
            return slab["i"]
        hg_E(0)
        if self.stop == "hgE0":
            self.tap("QT0", sets[0]["QT"], 512, BF16)
            self.tap("KT0", sets[0]["KT"], 512, BF16)
            self.tap("A30", sets[0]["A3"], 512)
            self.tap("EE0", sets[0]["EE"], 24)
            self.tap("VTM0", sets[0]["VTM"], 512, BF16)
            for i in hslabs[0]:
                self.ring_rel(i)
            return slab["i"]
        nun = 16
        if self.stop is not None and self.stop.startswith("hgU"):
            nun = int(self.stop[3:])
        if self.stop in ("smA", "smB", "smB1", "smC", "smD"):
            hg_small(0)
            self.tap("KTM0", sets[0]["KTM"], 512, BF16)
            self.tap("PSB0", sets[0]["PSB"], 256, BF16)
            self.tap("S1B0", sets[0]["S1B"], 1024, BF16)
            self.tap("Sst", SR[3], 128)
            self.tap("oloc", self.f32(a_OLOC, 8192), 8192)
            for i in hslabs[0]:
                self.ring_rel(i)
            return slab["i"]
        for u in range(nun):
            if u + 1 < 16:
                hg_P(u + 1, 0)
            hg_small(u, 1)
            if u + 1 < 16:
                hg_P(u + 1, 1)
            hg_small(u, 2)
            if u + 1 < 16:
                hg_E(u + 1)
        if nun < 16:
            self.tap("oloc", self.f32(a_OLOC, 8192), 8192)
            for hd in list(hslabs.keys()):
                if 2 * hd + 1 > nun:
                    for i in hslabs[hd]:
                        if i in self.r_slot:
                            self.ring_rel(i)
            return slab["i"]
        self.tap("oloc", self.f32(a_OLOC, 8192), 8192)
        self.tap("send", SEND, 1024)
        if self.stop == "hgrn":
            return slab["i"]
        rs1, rr1 = Reg("d_s1", 0, 1), Reg("d_r1", 0, 1)
        self.op("sp", lambda e: e.dma_start(out=self.d_s1[:, :], in_=SEND.ap), reads=[SEND], writes=[rs1], dma_key="snd1")
        self.op("pool", lambda e: e.collective_compute("AllGather", ALU.bypass, replica_groups=[[0, 1], [2, 3], [4, 5], [6, 7]],
                                                       ins=[self.d_s1.ap().opt()], outs=[self.d_r1.ap().opt()]),
                reads=[rs1], writes=[rr1], dma_key="cc1", dma_inc=1)

        if self.stop == "x1":
            RECVt = self.f32(a_SCR, 1024)
            self.op("sp", lambda e: e.dma_start(out=RECVt.ap, in_=self.d_r1[0:128, :]), reads=[rr1], writes=[RECVt], dma_key="rcv1")
            self.tap("recv", RECVt, 1024)
            return slab["i"]
        p = a_SCR
        csets = []
        for s in range(2):
            d = {}
            d["VS"] = self.f32(p, 512); p += 512
            d["U2"] = self.f32(p, 514); p += 514
            d["YA"] = self.f32(p, 512); p += 512
            d["SQ"] = self.bf(p, 256); p += 256
            csets.append(d)
        YPRE = self.f32(p, 16); p += 16
        BH = self.f32(p, 16); p += 16
        SEND2 = self.f32(p, 16); p += 16
        assert p <= A_END, p
        YPREv = YPRE.ap.rearrange("p (c t) -> p c t", t=2)
        BHv = BH.ap.rearrange("p (c t) -> p c t", t=2)
        SEND2v = SEND2.ap.rearrange("p (c t) -> p c t", t=2)
        cslabs = {}
        eps = self.prm(PC_EPS)

        def cv_P(u):
            c, th = divmod(u, 2)
            s = u % 2
            if th == 0:
                cslabs[c] = (take(), take(), take())
            ids = cslabs[c]
            for j in range(3):
                sl = self.ring_get(ids[j])
                s3 = sl.ap.rearrange("p (k n) -> p k n", n=128)
                b = self.bank(4 * s + j)
                for kc in range(16):
                    xk = self.Xc(kc)
                    self.mm(b.ap, b, s3[:, kc, :], xk.ap[:, th * TH:(th + 1) * TH], [sl, xk], kc == 0, kc == 15)
            if th == 1:
                for i in ids:
                    self.ring_rel(i)

        def cv_E(u):
            c, th = divmod(u, 2)
            s = u % 2
            d, dp = csets[s], csets[1 - s]
            bB, bC, bv, bn = self.bank(4 * s), self.bank(4 * s + 1), self.bank(4 * s + 2), self.bank(4 * s + 3)
            VS, U2, YA, SQ = d["VS"], d["U2"], d["YA"], d["SQ"]
            w0, w1, w2 = self.prm(PC_CW + c), self.prm(PC_CW + 8 + c), self.prm(PC_CW + 16 + c)
            cvn = self.prm(PC_CVN + c)
            op = self.op
            op("act", lambda e: e.activation(out=VS.ap, in_=bv.ap, func=AF.Copy), reads=[bv], writes=[VS])
            if th == 0:
                op("dve", lambda e: e.memset(U2.ap[:, 0:2], 0.0), writes=[U2])
            else:
                pu = dp["U2"]
                op("dve", lambda e: e.tensor_copy(out=U2.ap[:, 0:2], in_=pu.ap[:, 512:514]), reads=[pu], writes=[U2])
            op("dve", lambda e: e.tensor_tensor(out=U2.ap[:, 2:514], in0=bC.ap, in1=VS.ap, op=ALU.mult), reads=[bC, VS], writes=[U2])
            op("dve", lambda e: e.tensor_scalar(out=YA.ap, in0=U2.ap[:, 0:512], scalar1=w0.ap, scalar2=None, op0=ALU.mult), reads=[U2, w0], writes=[YA])
            op("dve", lambda e: e.scalar_tensor_tensor(out=YA.ap, in0=U2.ap[:, 1:513], scalar=w1.ap, in1=YA.ap, op0=ALU.mult, op1=ALU.add), reads=[U2, w1, YA], writes=[YA])
            op("dve", lambda e: e.scalar_tensor_tensor(out=YA.ap, in0=U2.ap[:, 2:514], scalar=w2.ap, in1=YA.ap, op0=ALU.mult, op1=ALU.add), reads=[U2, w2, YA], writes=[YA])
            if th == 0:
                op("dve", lambda e: e.tensor_copy(out=YPREv[:, c, :], in_=YA.ap[:, 0:2]), reads=[YA], writes=[YPRE])
                op("dve", lambda e: e.tensor_copy(out=BHv[:, c, :], in_=bB.ap[:, 0:2]), reads=[bB], writes=[BH])
            else:
                op("dve", lambda e: e.tensor_copy(out=SEND2v[:, c, :], in_=U2.ap[:, 512:514]), reads=[U2], writes=[SEND2])
            op("dve", lambda e: e.tensor_tensor(out=YA.ap, in0=YA.ap, in1=bB.ap, op=ALU.mult), reads=[YA, bB], writes=[YA])
            op("act", lambda e: e.activation(out=SQ.ap, in_=YA.ap, func=AF.Square), reads=[YA], writes=[SQ])
            self.mm(bn.ap, bn, self.ONESB.ap, SQ.ap, [self.ONESB, SQ], True, True)
            op("act", lambda e: e.activation(out=VS.ap, in_=bn.ap, func=AF.Ln, scale=1.0 / 128, bias=eps.ap), reads=[bn, eps], writes=[VS])
            op("act", lambda e: e.activation(out=VS.ap, in_=VS.ap, func=AF.Exp, scale=-0.5), reads=[VS], writes=[VS])
            yv = Yv(c, th)
            op("dve", lambda e: e.scalar_tensor_tensor(out=yv.ap, in0=YA.ap, scalar=cvn.ap, in1=VS.ap, op0=ALU.mult, op1=ALU.mult), reads=[YA, cvn, VS], writes=[yv])

        cv_P(0)
        for u in range(16):
            if u + 1 < 16:
                cv_P(u + 1)
            cv_E(u)
        if self.stop == "conv":
            return slab["i"]
        rs2, rr2 = Reg("d_s2", 0, 1), Reg("d_r2", 0, 1)
        self.op("sp", lambda e: e.dma_start(out=self.d_s2[:, :], in_=SEND2.ap), reads=[SEND2], writes=[rs2], dma_key="snd2")
        self.op("pool", lambda e: e.collective_compute("AllGather", ALU.bypass, replica_groups=[[0, 1], [2, 3], [4, 5], [6, 7]],
                                                       ins=[self.d_s2.ap().opt()], outs=[self.d_r2.ap().opt()]),
                reads=[rs2], writes=[rr2], dma_key="cc2", dma_inc=1)

        p = a_SCR
        psets = []
        for s in range(4):
            d = {}
            d["SQ"] = self.bf(p, 256); p += 256
            psets.append(d)
        SINB = self.bf(p, 512); p += 512
        RECV = self.f32(p, 1024); p += 1024
        RECV2 = self.f32(p, 16); p += 16
        UP = self.f32(p, 16); p += 16
        YF = self.f32(p, 16); p += 16
        TQ = self.f32(p, 24); p += 24
        RH = self.f32(p, 16); p += 16
        SQH = self.bf(p, 8); p += 8
        assert p <= YPRE.reg.lo, (p, YPRE.reg.lo)
        flag = self.prm(PC_FLAG)
        hgn = self.prm(PC_HGN)
        self.op("sp", lambda e: e.dma_start(out=RECV.ap, in_=self.d_r1[0:128, :]), reads=[rr1], writes=[RECV], dma_key="rcv1")
        SINv = SINB.ap.rearrange("p (h v) -> p h v", v=128)
        for hd in range(8):
            self.op("act", lambda e, hd=hd: e.activation(out=SINv[:, hd, :], in_=RECV.ap[:, hd * 128:(hd + 1) * 128], func=AF.Copy, scale=flag.ap),
                    reads=[RECV, flag], writes=[SINB])
        gsl = {}

        def po_P(u):
            hd, th = divmod(u, 2)
            s = u % 2
            if th == 0:
                gsl[hd] = take()
            sl = self.ring_get(gsl[hd])
            s3 = sl.ap.rearrange("p (k n) -> p k n", n=128)
            s = u % 4
            bg, bc = self.bank(2 * s), self.bank(2 * s + 1)
            for kc in range(16):
                xk = self.Xc(kc)
                self.mm(bg.ap, bg, s3[:, kc, :], xk.ap[:, th * TH:(th + 1) * TH], [sl, xk], kc == 0, kc == 15)
            qh = QH(hd, th)
            self.mm(bc.ap, bc, SINv[:, hd, :], qh.ap, [SINB, qh], True, True)
            if th == 1:
                self.ring_rel(gsl[hd])

        def po_E(u, part):
            hd, th = divmod(u, 2)
            s = u % 2
            s = u % 4
            d = psets[s]
            SQ = d["SQ"]
            bg, bc = self.bank(2 * s), self.bank(2 * s + 1)
            bn = bc
            R1 = bc
            ol = OLOC(hd, th)
            qh = QH(hd, th)
            op = self.op
            if part == 1:
                op("act", lambda e: e.activation(out=bg.ap, in_=bg.ap, func=AF.Silu), reads=[bg], writes=[bg])
                op("dve", lambda e: e.tensor_tensor(out=ol.ap, in0=ol.ap, in1=bc.ap, op=ALU.add), reads=[ol, bc], writes=[ol])
                op("act", lambda e: e.activation(out=SQ.ap, in_=ol.ap, func=AF.Square), reads=[ol], writes=[SQ])
                self.mm(bn.ap, bn, self.ONESB.ap, SQ.ap, [self.ONESB, SQ], True, True)
                return
            op("act", lambda e: e.activation(out=R1.ap, in_=bn.ap, func=AF.Ln, scale=1.0 / 128, bias=eps.ap), reads=[bn, eps], writes=[R1])
            op("act", lambda e: e.activation(out=R1.ap, in_=R1.ap, func=AF.Exp, scale=-0.5), reads=[R1], writes=[R1])
            op("dve", lambda e: e.scalar_tensor_tensor(out=ol.ap, in0=ol.ap, scalar=hgn.ap, in1=R1.ap, op0=ALU.mult, op1=ALU.mult), reads=[ol, hgn, R1], writes=[ol])
            op("dve", lambda e: e.tensor_tensor(out=qh.ap, in0=ol.ap, in1=bg.ap, op=ALU.mult), reads=[ol, bg], writes=[qh])

        for u in range(3):
            po_P(u)
        po_E(0, 1)
        for u in range(16):
            if u + 3 < 16:
                po_P(u + 3)
            if u + 1 < 16:
                po_E(u + 1, 1)
            po_E(u, 2)

        if self.stop == "post":
            return slab["i"]
        op = self.op
        op("sp", lambda e: e.dma_start(out=RECV2.ap, in_=self.d_r2[0:128, :]), reads=[rr2], writes=[RECV2], dma_key="rcv2")
        op("dve", lambda e: e.tensor_scalar(out=UP.ap, in0=RECV2.ap, scalar1=flag.ap, scalar2=None, op0=ALU.mult), reads=[RECV2, flag], writes=[UP])
        UPv = UP.ap.rearrange("p (c t) -> p c t", t=2)
        YFv = YF.ap.rearrange("p (c t) -> p c t", t=2)
        TQv = TQ.ap.rearrange("p (j c) -> p j c", c=8)
        CW0 = self.prm(PC_CW, 8)
        CW1 = self.prm(PC_CW + 8, 8)
        CVN = self.prm(PC_CVN, 8)
        op("dve", lambda e: e.tensor_tensor(out=TQv[:, 0, :], in0=UPv[:, :, 0], in1=CW0.ap, op=ALU.mult), reads=[UP, CW0], writes=[TQ])
        op("dve", lambda e: e.tensor_tensor(out=TQv[:, 1, :], in0=UPv[:, :, 1], in1=CW1.ap, op=ALU.mult), reads=[UP, CW1], writes=[TQ])
        op("dve", lambda e: e.tensor_tensor(out=TQv[:, 2, :], in0=UPv[:, :, 1], in1=CW0.ap, op=ALU.mult), reads=[UP, CW0], writes=[TQ])
        op("dve", lambda e: e.tensor_tensor(out=YFv[:, :, 0], in0=YPREv[:, :, 0], in1=TQv[:, 0, :], op=ALU.add), reads=[YPRE, TQ], writes=[YF])
        op("dve", lambda e: e.tensor_tensor(out=YFv[:, :, 0], in0=YFv[:, :, 0], in1=TQv[:, 1, :], op=ALU.add), reads=[YF, TQ], writes=[YF])
        op("dve", lambda e: e.tensor_tensor(out=YFv[:, :, 1], in0=YPREv[:, :, 1], in1=TQv[:, 2, :], op=ALU.add), reads=[YPRE, TQ], writes=[YF])
        op("dve", lambda e: e.tensor_tensor(out=YF.ap, in0=YF.ap, in1=BH.ap, op=ALU.mult), reads=[YF, BH], writes=[YF])
        op("act", lambda e: e.activation(out=SQH.ap, in_=YF.ap, func=AF.Square), reads=[YF], writes=[SQH])
        bh = self.bank(3)
        self.mm(bh.ap[:, 0:16], bh, self.ONESB.ap, SQH.ap, [self.ONESB, SQH], True, True)
        op("act", lambda e: e.activation(out=RH.ap, in_=bh.ap[:, 0:16], func=AF.Ln, scale=1.0 / 128, bias=eps.ap), reads=[bh, eps], writes=[RH])
        op("act", lambda e: e.activation(out=RH.ap, in_=RH.ap, func=AF.Exp, scale=-0.5), reads=[RH], writes=[RH])
        op("dve", lambda e: e.tensor_tensor(out=YF.ap, in0=YF.ap, in1=RH.ap, op=ALU.mult), reads=[YF, RH], writes=[YF])
        op("dve", lambda e: e.tensor_tensor(out=YFv, in0=YFv, in1=CVN.ap.unsqueeze(2).broadcast_to([128, 8, 2]), op=ALU.mult), reads=[YF, CVN], writes=[YF])
        Yall = self.bf(a_Y, 4096)
        Yallv = Yall.ap.rearrange("p (c t) -> p c t", t=1024)
        op("dve", lambda e: e.tensor_copy(out=Yallv[:, :, 0:2], in_=YFv), reads=[YF], writes=[Yall])
        self.tap("ycv", Yall, 8192, BF16)
        self.tap("ohg", self.bf(a_QH, 4096), 8192, BF16)

        if self.stop == "halo":
            return slab["i"]
        bi = 0
        for n in range(16):
            i = take()
            sl = self.ring_get(i)
            s3 = sl.ap.rearrange("p (k n) -> p k n", n=128)
            for th in range(2):
                b = self.bank(bi % 8)
                bi += 1
                for kc in range(16):
                    src = QH(kc, th) if kc < 8 else Yv(kc - 8, th)
                    self.mm(b.ap, b, s3[:, kc, :], src.ap, [sl, src], kc == 0, kc == 15)
                hv = V(self.Hc(n).ap[:, th * TH:(th + 1) * TH], Reg("sb", A_H + n * 1024 + th * TH, A_H + n * 1024 + (th + 1) * TH))
                self.op("dve", lambda e, hv=hv, b=b: e.tensor_tensor(out=hv.ap, in0=hv.ap, in1=b.ap, op=ALU.add), reads=[hv, b], writes=[hv])
            self.ring_rel(i)
        return slab["i"]

    def finish(self):
        nc = self.nc
        with contextlib.ExitStack() as st:
            sems = {e: st.enter_context(nc.semaphore("s_" + e)) for e in ENGS}
            dsem = {k: st.enter_context(nc.semaphore("d_" + k)) for k in self.S.dma_tot.keys()}
            block = st.enter_context(nc.Block())
            be = {"pe": block.tensor, "act": block.scalar, "dve": block.vector, "pool": block.gpsimd, "sp": block.sync}
            self.stats = self.S.emit(be, sems, dsem)
        self.stack.close()
        return nc


def _consts():
    cst = np.zeros((128, NCST), np.float32)
    cst[:, 0:128] = np.eye(128, dtype=np.float32)
    cst[:, 128:256] = 1.0
    pp = np.arange(128)[:, None] % 64
    tt = np.arange(256)[None, :] % 64
    cst[:, 256:512] = (pp <= tt).astype(np.float32)
    return cst


def _prm(I, core):
    prm = np.zeros((128, NPRM_IN), np.float32)

    def col16(v):
        return np.asarray(v, np.float32).reshape(-1, 128).T

    prm[:, PC_G1:PC_G1 + 16] = col16(I["norm_ffn1"][0])
    prm[:, PC_GM:PC_GM + 16] = col16(I["norm_mix"][0])
    prm[:, PC_G2:PC_G2 + 16] = col16(I["norm_ffn2"][0])
    prm[:, PC_GP:PC_GP + 16] = col16(I["norm_ple"][0])
    prm[:, PC_GF:PC_GF + 16] = col16(I["norm_final"])
    prm[:, PC_L0:PC_L0 + 8] = col16(I["hgrn_lb_logits"][0])
    prm[:, PC_L1:PC_L1 + 8] = col16(I["hgrn_lb_logits"][1])
    prm[:, PC_HGN] = np.asarray(I["hgrn_norm"][0], np.float32)
    prm[:, PC_CVN:PC_CVN + 8] = col16(I["conv_norm"][0])
    for j in range(3):
        prm[:, PC_CW + 8 * j:PC_CW + 8 * j + 8] = col16(I["conv_w"][0][j])
    prm[:, PC_FLAG] = float(core % 2)
    prm[:, PC_EPS] = EPS
    return prm


_CACHE = {}


STOP = None
SKIP1 = False


def _get_program(debug=False):
    key = ("prog", debug, STOP, SKIP1)
    if key not in _CACHE:
        ns = None
        if STOP is not None:
            b0 = Builder(debug=False, stop=STOP, skip1=SKIP1)
            b0.build()
            ns = b0.used_slabs
            b0.stack.close()
        b = Builder(debug=debug, stop=STOP, nslab=ns, skip1=SKIP1)
        b.build()
        b.finish()
        _CACHE[key] = b
    return _CACHE[key]


def kernel(x, p, norm_ffn1, ffn1_gate, ffn1_up, ffn1_down, norm_mix, w_in, conv_w, hgrn_lb_logits, hgrn_norm,
           conv_norm, w_out, norm_ffn2, ffn2_gate, ffn2_up, ffn2_down, norm_ple, w_ple, w_ple_gate, norm_final,
           _debug=False):
    I = dict(norm_ffn1=np.asarray(norm_ffn1), norm_mix=np.asarray(norm_mix), norm_ffn2=np.asarray(norm_ffn2),
             norm_ple=np.asarray(norm_ple), norm_final=np.asarray(norm_final), hgrn_lb_logits=np.asarray(hgrn_lb_logits),
             hgrn_norm=np.asarray(hgrn_norm), conv_norm=np.asarray(conv_norm), conv_w=np.asarray(conv_w))
    W = {"ffn1_gate": np.asarray(ffn1_gate)[0], "ffn1_up": np.asarray(ffn1_up)[0], "ffn1_down": np.asarray(ffn1_down)[0],
         "ffn2_gate": np.asarray(ffn2_gate)[0], "ffn2_up": np.asarray(ffn2_up)[0], "ffn2_down": np.asarray(ffn2_down)[0],
         "w_in": np.asarray(w_in)[0], "w_out": np.asarray(w_out)[0], "w_ple": np.asarray(w_ple)[0],
         "w_ple_gate": np.asarray(w_ple_gate)[0]}
    b = _get_program(_debug)
    wts = build_slabs(b.plan[b.slab_off:b.nslab], W)
    cst = _consts()
    x = np.asarray(x, np.float32)
    p = np.asarray(p, np.float32)
    in_maps = []
    for c in range(NCORES):
        bi, half = divmod(c, 2)
        xs = x[bi, half * T:(half + 1) * T, :]
        xT = np.ascontiguousarray(xs.reshape(T, 16, 128).transpose(2, 1, 0)).reshape(128, 16 * T)
        ps_ = p[0, bi, half * T:(half + 1) * T, :]
        pT = np.ascontiguousarray(ps_.reshape(T, 2, 128).transpose(2, 1, 0)).reshape(128, 2 * T)
        in_maps.append({"xT": xT, "pT": pT, "prm": _prm(I, c), "cst": cst, "wts": wts})
    res = run_bass_kernel_spmd(b.nc, in_maps, core_ids=list(range(NCORES)))
    out = np.empty((4, 2 * T, D), np.float32)
    for c in range(NCORES):
        bi, half = divmod(c, 2)
        yT = res.results[c]["yT"].reshape(128, 16, T)
        out[bi, half * T:(half + 1) * T, :] = yT.transpose(2, 1, 0).reshape(T, D)
    if _debug:
        return out, res.results
    return out
```
